# Optimizing a Trainium2 kernel written in Bass

```python
import jax, jax.numpy as jnp
from jax import lax
import numpy as np

D_MODEL = 2048
BATCH = 2
SEQ = 8192
DEPTH = 1
DEC_BATCH = 16
DEC_SEQ = 32
PAST_LEN = 4096

CHUNK = 64
RNN_WIDTH = 1024
RNN_BLOCKS = 16
RNN_BW = RNN_WIDTH // RNN_BLOCKS
RNN_CONV_W = 4
LRU_C = 8.0
HG_HEADS = 8
HG_DK = 128
HG_DV = 128
HG_WIDTH = HG_HEADS * HG_DK
MEM_LEN = 256
XA_HEADS = 4
XA_HD = 256
XA_WIDTH = XA_HEADS * XA_HD
N_BRANCH = 3
BRANCH_WIDTH = 1024
FFN_DIM = 5632
FFN_CONV_W = 3
EPS = 1e-6

OFF_RNN = 0
OFF_HQ = OFF_RNN + RNN_WIDTH
OFF_HF = OFF_HQ + HG_WIDTH
OFF_HI = OFF_HF + HG_WIDTH
OFF_HO = OFF_HI + HG_HEADS * HG_DV
OFF_XQ = OFF_HO + HG_HEADS * HG_DV
OFF_GATE = OFF_XQ + XA_WIDTH
IN_COLS = OFF_GATE + N_BRANCH * D_MODEL

kernel_name = "hawk_hgrn2_memxattn_convffn_stream_step"

F32 = jnp.float32


def rmsnorm(x, g):
    xf = x.astype(F32)
    y = xf * lax.rsqrt(jnp.mean(xf * xf, axis=-1, keepdims=True) + EPS)
    return (y * g.astype(F32)).astype(x.dtype)


def causal_dwconv(x, prev, w, b):
    width = w.shape[0]
    L = x.shape[1]
    xp = jnp.concatenate([prev.astype(x.dtype), x], axis=1)
    y = b
    for j in range(width):
        y = y + xp[:, j:j + L] * w[j]
    return y.astype(x.dtype), xp[:, L:]


def _lin_combine(left, right):
    a1, b1 = left
    a2, b2 = right
    return a1 * a2, a2 * b1 + b2


def rg_lru(x, h0, wa, ba, wx, bx, lam):
    B, L, C = x.shape
    xf = x.astype(F32)
    xb = xf.reshape(B, L, RNN_BLOCKS, RNN_BW)
    r = jax.nn.sigmoid(jnp.einsum('blhi,hij->blhj', xb, wa.astype(F32)) + ba.astype(F32)).reshape(B, L, C)
    ig = jax.nn.sigmoid(jnp.einsum('blhi,hij->blhj', xb, wx.astype(F32)) + bx.astype(F32)).reshape(B, L, C)
    log_a = -LRU_C * r * jax.nn.softplus(-lam.astype(F32))
    a = jnp.exp(log_a)
    b = jnp.sqrt(-jnp.expm1(2.0 * log_a)) * (ig * xf)
    b = b.at[:, 0].add(a[:, 0] * h0.astype(F32))
    _, h = lax.associative_scan(_lin_combine, (a, b), axis=1)
    return h.astype(x.dtype), h[:, -1].astype(x.dtype)


def hgrn2(q, f_raw, i, S0, lb):
    B, L, _ = q.shape
    c = min(CHUNK, L)
    n = L // c
    f = lb + (1.0 - lb) * jax.nn.sigmoid(f_raw.astype(F32))
    g = jnp.log(f)
    k = 1.0 - f

    def blocks(t, d):
        return t.astype(F32).reshape(B, n, c, HG_HEADS, d).transpose(1, 0, 3, 2, 4)

    qs, ks, gs, vs = blocks(q, HG_DK), blocks(k, HG_DK), blocks(g, HG_DK), blocks(i, HG_DV)
    tri = jnp.tril(jnp.ones((c, c), dtype=bool))[:, :, None]

    def step(S, inp):
        qc, kc, vc, gc = inp
        bcum = jnp.cumsum(gc, axis=2)
        o_inter = jnp.einsum('bhtk,bhkv->bhtv', qc * jnp.exp(bcum), S)
        diff = bcum[:, :, :, None, :] - bcum[:, :, None, :, :]
        decay = jnp.where(tri, jnp.exp(jnp.minimum(diff, 0.0)), 0.0)
        A = jnp.einsum('bhtk,bhsk,bhtsk->bhts', qc, kc, decay)
        o = o_inter + jnp.einsum('bhts,bhsv->bhtv', A, vc)
        blast = bcum[:, :, -1:, :]
        S_new = jnp.exp(blast[:, :, 0])[..., None] * S + jnp.einsum(
            'bhsk,bhsv->bhkv', kc * jnp.exp(blast - bcum), vc)
        return S_new, o

    S, o = lax.scan(step, S0.astype(F32), (qs, ks, vs, gs))
    o = o.transpose(1, 0, 3, 2, 4).reshape(B, L, HG_HEADS, HG_DV)
    return o, S


def memory_kv(mem, g, w_kv):
    B, M, _ = mem.shape
    kv = rmsnorm(mem, g) @ w_kv
    k = kv[..., :XA_WIDTH].reshape(B, M, XA_HEADS, XA_HD)
    v = kv[..., XA_WIDTH:].reshape(B, M, XA_HEADS, XA_HD)
    return k, v


def mem_cross_attn(q, mk, mv):
    B, L, _ = q.shape
    qh = q.reshape(B, L, XA_HEADS, XA_HD)
    s = jnp.einsum('blhd,bmhd->bhlm', qh, mk.astype(q.dtype)).astype(F32) * (XA_HD ** -0.5)
    p = jax.nn.softmax(s, axis=-1).astype(q.dtype)
    return jnp.einsum('bhlm,bmhd->blhd', p, mv.astype(q.dtype)).reshape(B, L, XA_WIDTH)


def trunk_layer(x, mk, mv, h0, rconv0, S0, fconv0, p, lb):
    B, L, _ = x.shape
    xn = rmsnorm(x, p['pre_mix_norm'])
    z = xn @ p['w_in']
    xr, rconv1 = causal_dwconv(z[..., OFF_RNN:OFF_HQ], rconv0, p['rnn_conv_w'], p['rnn_conv_b'])
    y_a, h1 = rg_lru(xr, h0, p['lru_wa'], p['lru_ba'], p['lru_wx'], p['lru_bx'], p['lru_lambda'])
    o, S1 = hgrn2(z[..., OFF_HQ:OFF_HF], z[..., OFF_HF:OFF_HI], z[..., OFF_HI:OFF_HO], S0, lb)
    og = jax.nn.sigmoid(z[..., OFF_HO:OFF_XQ].astype(F32)).reshape(B, L, HG_HEADS, HG_DV)
    y_b = (rmsnorm(o, p['hg_norm']) * og).reshape(B, L, HG_HEADS * HG_DV).astype(x.dtype)
    y_c = mem_cross_attn(z[..., OFF_XQ:OFF_GATE], mk, mv)
    branches = (y_a, y_b, y_c)
    m = jnp.zeros((B, L, D_MODEL), F32)
    for nb in range(N_BRANCH):
        gl = z[..., OFF_GATE + nb * D_MODEL:OFF_GATE + (nb + 1) * D_MODEL] + p['b_gate'][nb]
        m = m + jax.nn.sigmoid(gl.astype(F32)) * (branches[nb] @ p['w_branch'][nb]).astype(F32)
    y = m.astype(x.dtype) @ p['w_out']
    x = x + rmsnorm(y, p['post_mix_norm'])
    hf = rmsnorm(x, p['pre_ffn_norm']) @ p['w_ffn_up']
    u, fconv1 = causal_dwconv(hf[..., :FFN_DIM], fconv0, p['ffn_conv_w'], p['ffn_conv_b'])
    act = jax.nn.gelu(u) * hf[..., FFN_DIM:]
    x = x + rmsnorm(act @ p['w_ffn_down'], p['post_ffn_norm'])
    return x, h1, rconv1, S1.astype(x.dtype), fconv1


def setup_inputs(seed: int = 0) -> dict:
    key = jax.random.key(seed)
    kit = iter(jax.random.split(key, 48))

    def nrm(shape, scale):
        return jax.random.normal(next(kit), shape, F32) * scale

    def gain(shape):
        return 1.0 + nrm(shape, 0.05)

    u = jax.random.uniform(next(kit), (DEPTH, RNN_WIDTH), F32, minval=0.9, maxval=0.999)
    s = u ** (1.0 / LRU_C)
    lam = jnp.log(s) - jnp.log1p(-s)
    return {
        "x_prompt": nrm((BATCH, SEQ, D_MODEL), 1.0),
        "x_sample": nrm((DEC_BATCH, DEC_SEQ, D_MODEL), 1.0),
        "cache_mem_k": nrm((DEPTH, DEC_BATCH, MEM_LEN, XA_HEADS, XA_HD), 1.0),
        "cache_mem_v": nrm((DEPTH, DEC_BATCH, MEM_LEN, XA_HEADS, XA_HD), 1.0),
        "state_rnn_h": nrm((DEPTH, DEC_BATCH, RNN_WIDTH), 0.5),
        "state_rnn_conv": nrm((DEPTH, DEC_BATCH, RNN_CONV_W - 1, RNN_WIDTH), 1.0),
        "state_hg": nrm((DEPTH, DEC_BATCH, HG_HEADS, HG_DK, HG_DV), 0.5),
        "state_ffn_conv": nrm((DEPTH, DEC_BATCH, FFN_CONV_W - 1, FFN_DIM), 1.0),
        "mem_prompt": nrm((BATCH, MEM_LEN, D_MODEL), 1.0),
        "pre_mix_norm": gain((DEPTH, D_MODEL)),
        "w_in": nrm((DEPTH, D_MODEL, IN_COLS), D_MODEL ** -0.5),
        "rnn_conv_w": nrm((DEPTH, RNN_CONV_W, RNN_WIDTH), RNN_CONV_W ** -0.5),
        "rnn_conv_b": nrm((DEPTH, RNN_WIDTH), 0.01),
        "lru_wa": nrm((DEPTH, RNN_BLOCKS, RNN_BW, RNN_BW), RNN_BW ** -0.5),
        "lru_ba": nrm((DEPTH, RNN_BLOCKS, RNN_BW), 0.01),
        "lru_wx": nrm((DEPTH, RNN_BLOCKS, RNN_BW, RNN_BW), RNN_BW ** -0.5),
        "lru_bx": nrm((DEPTH, RNN_BLOCKS, RNN_BW), 0.01),
        "lru_lambda": lam,
        "hg_lb": nrm((DEPTH + 1, HG_WIDTH), 0.5),
        "hg_norm": gain((DEPTH, HG_DV)),
        "mem_norm": gain((DEPTH, D_MODEL)),
        "w_mem_kv": nrm((DEPTH, D_MODEL, 2 * XA_WIDTH), D_MODEL ** -0.5),
        "w_branch": nrm((DEPTH, N_BRANCH, BRANCH_WIDTH, D_MODEL), BRANCH_WIDTH ** -0.5),
        "b_gate": nrm((DEPTH, N_BRANCH, D_MODEL), 0.01),
        "w_out": nrm((DEPTH, D_MODEL, D_MODEL), D_MODEL ** -0.5),
        "post_mix_norm": gain((DEPTH, D_MODEL)),
        "pre_ffn_norm": gain((DEPTH, D_MODEL)),
        "w_ffn_up": nrm((DEPTH, D_MODEL, 2 * FFN_DIM), D_MODEL ** -0.5),
        "ffn_conv_w": nrm((DEPTH, FFN_CONV_W, FFN_DIM), FFN_CONV_W ** -0.5),
        "ffn_conv_b": nrm((DEPTH, FFN_DIM), 0.01),
        "w_ffn_down": nrm((DEPTH, FFN_DIM, D_MODEL), FFN_DIM ** -0.5),
        "post_ffn_norm": gain((DEPTH, D_MODEL)),
    }


def reference(x_prompt, x_sample, cache_mem_k, cache_mem_v, state_rnn_h, state_rnn_conv, state_hg,
              state_ffn_conv, mem_prompt, pre_mix_norm, w_in, rnn_conv_w, rnn_conv_b, lru_wa, lru_ba,
              lru_wx, lru_bx, lru_lambda, hg_lb, hg_norm, mem_norm, w_mem_kv, w_branch, b_gate, w_out,
              post_mix_norm, pre_ffn_norm, w_ffn_up, ffn_conv_w, ffn_conv_b, w_ffn_down, post_ffn_norm):
    lb_all = jnp.cumsum(jax.nn.softmax(hg_lb.astype(F32), axis=0), axis=0)
    yp, ys = x_prompt, x_sample
    dt = x_prompt.dtype
    mk_p_l, mv_p_l, hp_l, rcp_l, sp_l, fcp_l = [], [], [], [], [], []
    hs_l, rcs_l, ss_l, fcs_l = [], [], [], []
    for l in range(DEPTH):
        p = {
            'pre_mix_norm': pre_mix_norm[l], 'w_in': w_in[l], 'rnn_conv_w': rnn_conv_w[l],
            'rnn_conv_b': rnn_conv_b[l], 'lru_wa': lru_wa[l], 'lru_ba': lru_ba[l], 'lru_wx': lru_wx[l],
            'lru_bx': lru_bx[l], 'lru_lambda': lru_lambda[l], 'hg_norm': hg_norm[l],
            'w_branch': w_branch[l], 'b_gate': b_gate[l], 'w_out': w_out[l],
            'post_mix_norm': post_mix_norm[l], 'pre_ffn_norm': pre_ffn_norm[l], 'w_ffn_up': w_ffn_up[l],
            'ffn_conv_w': ffn_conv_w[l], 'ffn_conv_b': ffn_conv_b[l], 'w_ffn_down': w_ffn_down[l],
            'post_ffn_norm': post_ffn_norm[l],
        }
        lb = lb_all[l]
        mk_p, mv_p = memory_kv(mem_prompt, mem_norm[l], w_mem_kv[l])
        B = yp.shape[0]
        yp, h_p, rc_p, s_p, fc_p = trunk_layer(
            yp, mk_p, mv_p, jnp.zeros((B, RNN_WIDTH), dt), jnp.zeros((B, RNN_CONV_W - 1, RNN_WIDTH), dt),
            jnp.zeros((B, HG_HEADS, HG_DK, HG_DV), dt), jnp.zeros((B, FFN_CONV_W - 1, FFN_DIM), dt), p, lb)
        ys, h_s, rc_s, s_s, fc_s = trunk_layer(
            ys, cache_mem_k[l], cache_mem_v[l], state_rnn_h[l], state_rnn_conv[l], state_hg[l],
            state_ffn_conv[l], p, lb)
        mk_p_l.append(mk_p); mv_p_l.append(mv_p); hp_l.append(h_p); rcp_l.append(rc_p)
        sp_l.append(s_p); fcp_l.append(fc_p)
        hs_l.append(h_s); rcs_l.append(rc_s); ss_l.append(s_s); fcs_l.append(fc_s)
    mem_k_prompt = jnp.stack(mk_p_l)
    mem_v_prompt = jnp.stack(mv_p_l)
    rnn_h_prompt = jnp.stack(hp_l)
    rnn_conv_prompt = jnp.stack(rcp_l)
    hg_prompt = jnp.stack(sp_l)
    ffn_conv_prompt = jnp.stack(fcp_l)
    rnn_h_sample = jnp.stack(hs_l)
    rnn_conv_sample = jnp.stack(rcs_l)
    hg_sample = jnp.stack(ss_l)
    ffn_conv_sample = jnp.stack(fcs_l)
    return (yp, ys, mem_k_prompt, mem_v_prompt, rnn_h_prompt, rnn_conv_prompt, hg_prompt, ffn_conv_prompt,
            rnn_h_sample, rnn_conv_sample, hg_sample, ffn_conv_sample)
```

```python
import numpy as np
import concourse.bass as bass
import concourse.mybir as mybir
from concourse.bass_utils import run_bass_kernel_spmd
from contextlib import ExitStack

F32 = mybir.dt.float32
BF16 = mybir.dt.bfloat16
AF = mybir.ActivationFunctionType
ALU = mybir.AluOpType
AX = mybir.AxisListType

ENGS = ("pe", "act", "dve", "pool", "sp")
N_DMA_HW = 16
N_DMA_SW = 8
N_DMA_SEMS = N_DMA_HW + N_DMA_SW
SAME_ENGINE_SYNC = True


class Buf:
    __slots__ = ("name", "last_w", "readers")

    def __init__(self, name=""):
        self.name = name
        self.last_w = None
        self.readers = []


class Op:
    __slots__ = ("eng", "fn", "deps", "needs_inc", "kind", "dma_sem", "dma_val", "idx", "incval")

    def __init__(self, eng, fn, kind):
        self.eng = eng
        self.fn = fn
        self.deps = []
        self.needs_inc = False
        self.kind = kind
        self.dma_sem = None
        self.dma_val = None
        self.idx = None
        self.incval = None


class Sched:
    def __init__(self):
        self.ops = {e: [] for e in ENGS}
        self.dma_count = 0
        self.dma_count_sw = 0
        self.dma_last = [None] * N_DMA_SEMS
        self.dma_vals = [0] * N_DMA_SEMS
        self.cc_val = 0
        self.cc_last = None
        self.fence = None
        self.fenced = set()

    def barrier(self):
        f = [self.ops[e][-1] for e in ENGS if self.ops[e]]
        f += [o for o in self.dma_last if o is not None]
        if self.cc_last is not None:
            f.append(self.cc_last)
        self.fence = f
        self.fenced = {"pool"}

    def _add(self, eng, fn, reads, writes, kind):
        op = Op(eng, fn, kind)
        op.idx = len(self.ops[eng])
        deps = []
        if self.fence is not None and eng not in self.fenced:
            deps.extend(self.fence)
            self.fenced.add(eng)
        for b in reads:
            if b.last_w is not None:
                deps.append(b.last_w)
        for b in writes:
            if b.last_w is not None:
                deps.append(b.last_w)
            deps.extend(b.readers)
        if kind == "dma":
            if eng == "pool":
                k = N_DMA_HW + self.dma_count_sw % N_DMA_SW
                self.dma_count_sw += 1
            else:
                k = self.dma_count % N_DMA_HW
                self.dma_count += 1
            prev = self.dma_last[k]
            if prev is not None:
                deps.append(prev)
            self.dma_vals[k] += 16
            op.dma_sem = k
            op.dma_val = self.dma_vals[k]
            self.dma_last[k] = op
        elif kind == "cc":
            if self.cc_last is not None:
                deps.append(self.cc_last)
            self.cc_val += 1
            op.dma_val = self.cc_val
            self.cc_last = op
        seen = set()
        best = {}
        for d in deps:
            if id(d) in seen or d is op:
                continue
            seen.add(id(d))
            if d.kind == "op":
                if d.eng == eng and (eng == "pe" or eng == "sp" or not SAME_ENGINE_SYNC):
                    continue
                if d.eng not in best or best[d.eng].idx < d.idx:
                    best[d.eng] = d
            else:
                op.deps.append(d)
        for d in best.values():
            d.needs_inc = True
            op.deps.append(d)
        self.ops[eng].append(op)
        for b in writes:
            b.last_w = op
            b.readers = []
        for b in reads:
            if b.last_w is not op:
                b.readers.append(op)
        return op

    def op(self, eng, fn, reads=(), writes=()):
        return self._add(eng, fn, reads, writes, "op")

    def dma(self, eng, fn, reads=(), writes=()):
        return self._add(eng, fn, reads, writes, "dma")

    def cc(self, fn, reads=(), writes=()):
        return self._add("pool", fn, reads, writes, "cc")

    def lower(self, nc, final_wait_eng="sp"):
        for e in ENGS:
            c = 0
            for op in self.ops[e]:
                if op.kind == "op" and op.needs_inc:
                    c += 1
                    op.incval = c
        with ExitStack() as es:
            esem = {e: es.enter_context(nc.semaphore("s_" + e)) for e in ENGS}
            dsem = [es.enter_context(nc.semaphore("d_%d" % i)) for i in range(N_DMA_SEMS)]
            csem = es.enter_context(nc.semaphore("ccs"))
            block = es.enter_context(nc.Block())
            sched = self

            def run(ename, engobj):
                waited_e = {a: 0 for a in ENGS}
                waited_d = [0] * N_DMA_SEMS
                waited_c = 0
                for op in sched.ops[ename]:
                    for d in op.deps:
                        if d.kind == "dma":
                            if waited_d[d.dma_sem] < d.dma_val:
                                engobj.wait_ge(dsem[d.dma_sem], d.dma_val)
                                waited_d[d.dma_sem] = d.dma_val
                        elif d.kind == "cc":
                            if waited_c < d.dma_val:
                                engobj.wait_ge(csem, d.dma_val)
                                waited_c = d.dma_val
                        else:
                            if waited_e[d.eng] < d.incval:
                                engobj.wait_ge(esem[d.eng], d.incval)
                                waited_e[d.eng] = d.incval
                    inst = op.fn(engobj)
                    if op.kind == "dma":
                        inst.then_inc(dsem[op.dma_sem], 16)
                    elif op.kind == "cc":
                        inst.then_inc(csem)
                    elif op.needs_inc:
                        inst.then_inc(esem[ename], 1)
                if ename == final_wait_eng:
                    for k in range(N_DMA_SEMS):
                        if sched.dma_vals[k] > 0:
                            engobj.wait_ge(dsem[k], sched.dma_vals[k])

            @block.tensor
            def _(e):
                run("pe", e)

            @block.scalar
            def _(e):
                run("act", e)

            @block.vector
            def _(e):
                run("dve", e)

            @block.gpsimd
            def _(e):
                run("pool", e)

            @block.sync
            def _(e):
                run("sp", e)


P = 128
D = 2048
KD = 16
NBLK = 4
PT = 512
ST = 16
TB = PT + ST
HALF = TB // 2
NS = ((0, HALF), (HALF, TB))
FFN = 5632
NF = 44
IN_COLS = 12288
EPS = 1e-6
NCORES = 8
TILE_ROWS = (128, 128, 128, 128, ST)
TILE_COL0 = (0, 128, 256, 384, 512)
SEGS = ((0, PT, 64), (PT, ST, ST))
SLOT = 544
NSLOT = 18
SUMW = 8 + 8 + 8 + 1024


class TT:
    def __init__(self, t, name=""):
        self.t = t
        self.b = Buf(name)

    def __getitem__(self, k):
        return self.t[k]


def build_program(debug=False):
    nc = bass.Bass("TRN2", target_bir_lowering=False)
    S = Sched()

    def din(name, shape):
        return nc.dram_tensor(name, list(shape), F32, kind="ExternalInput").ap()

    def dout(name, shape):
        return nc.dram_tensor(name, list(shape), F32, kind="ExternalOutput").ap()

    xp = din("xp", [NBLK * PT, D])
    xs = din("xs", [2 * 32, D])
    xhalo = din("xhalo", [3, D])
    mem = din("mem", [256, D])
    ck = din("ck", [2, 256, 1024])
    cv = din("cv", [2, 256, 1024])
    st_h = din("st_h", [2, 1024])
    st_conv = din("st_conv", [2, 3, 1024])
    st_hg = din("st_hg", [2, 8, 128, 128])
    st_fconv = din("st_fconv", [2, 2, FFN])
    onehot = din("onehot", [P, NCORES])
    predm = din("predm", [P, NCORES])
    prev1 = din("prev1", [P, NCORES])
    pre_mix_norm = din("pre_mix_norm", [1, D])
    w_in = din("w_in", [D, IN_COLS])
    rnn_conv_w = din("rnn_conv_w", [4, 1024])
    rnn_conv_b = din("rnn_conv_b", [1, 1024])
    lru_wa = din("lru_wa", [16, 64, 64])
    lru_ba = din("lru_ba", [1, 1024])
    lru_wx = din("lru_wx", [16, 64, 64])
    lru_bx = din("lru_bx", [1, 1024])
    lru_lambda = din("lru_lambda", [1, 1024])
    hg_lb = din("hg_lb", [2, 1024])
    hg_norm = din("hg_norm", [1, 128])
    mem_norm = din("mem_norm", [1, D])
    w_mem_kv = din("w_mem_kv", [D, 2048])
    w_branch = din("w_branch", [3, 1024, D])
    b_gate = din("b_gate", [3, D])
    w_out = din("w_out", [D, D])
    post_mix_norm = din("post_mix_norm", [1, D])
    pre_ffn_norm = din("pre_ffn_norm", [1, D])
    w_ffn_up = din("w_ffn_up", [D, 2 * FFN])
    ffn_conv_w = din("ffn_conv_w", [3, FFN])
    ffn_conv_b = din("ffn_conv_b", [1, FFN])
    w_ffn_down = din("w_ffn_down", [FFN, D])
    post_ffn_norm = din("post_ffn_norm", [1, D])

    y_p = dout("y_p", [NBLK * PT, D])
    y_s = dout("y_s", [64, D])
    mk_o = dout("mk_o", [256, 1024])
    mv_o = dout("mv_o", [256, 1024])
    h_p_o = dout("h_p_o", [1, 1024])
    conv_p_o = dout("conv_p_o", [3, 1024])
    hg_p_o = dout("hg_p_o", [8, 128, 128])
    fconv_p_o = dout("fconv_p_o", [2, FFN])
    h_s_o = dout("h_s_o", [2, 1024])
    conv_s_o = dout("conv_s_o", [2, 3, 1024])
    hg_s_o = dout("hg_s_o", [2, 8, 128, 128])
    fconv_s_o = dout("fconv_s_o", [2, 2, FFN])

    x1d = nc.dram_tensor("x1d", [NBLK * TB, D], F32).ap()
    ar1_in = nc.dram_tensor("ar1_in", [P, NCORES * SUMW], F32)
    ar1_out = nc.dram_tensor("ar1_out", [P, NCORES * SUMW], F32)
    ar2_in = nc.dram_tensor("ar2_in", [P, NCORES * 88], F32)
    ar2_out = nc.dram_tensor("ar2_out", [P, NCORES * 88], F32)

    es = ExitStack()
    with es:
        def sb(name, shape, dt=F32):
            return TT(es.enter_context(nc.sbuf_tensor(name, list(shape), dt)), name)

        xT = sb("xT", [P, KD, TB], BF16)
        A2 = sb("A2", [P, NF * TB], BF16)
        A3 = sb("A3", [P, 10240], F32)
        slabs = [sb("slab%d" % i, [P, 5632], BF16) for i in range(3)]
        kT_p = sb("kT_p", [P, 8, 256], BF16)
        v_p = sb("v_p", [P, 2, 1024], BF16)
        kT_s = sb("kT_s", [P, 8, 256], BF16)
        v_s = sb("v_s", [P, 2, 1024], BF16)
        xt = sb("xt", [P, D], F32)
        xsb = sb("xsb", [P, D], BF16)
        gpm = sb("gpm", [P, D], F32)
        gpf = sb("gpf", [P, D], F32)
        ident = sb("ident", [P, P], BF16)
        identf = sb("identf", [P, P], F32)
        ones_bf = sb("ones_bf", [P, P], BF16)
        triT = sb("triT", [64, 64], F32)
        rstm = sb("rstm", [P, TB], F32)
        g_pre = sb("g_pre", [P, KD], F32)
        g_ffn = sb("g_ffn", [P, KD], F32)
        g_mem = sb("g_mem", [P, KD], F32)
        cw = sb("cw", [P, 8, 4], F32)
        cb = sb("cb", [P, 8], F32)
        ba = sb("ba", [P, 8], F32)
        bx = sb("bx", [P, 8], F32)
        lamc = sb("lamc", [P, 8], F32)
        lamc2 = sb("lamc2", [P, 8], F32)
        lbt = sb("lbt", [P, 8], F32)
        omlb = sb("omlb", [P, 8], F32)
        lb2 = sb("lb2", [P, 2, 8], F32)
        hgn = sb("hgn", [P, 1], F32)
        bgt = sb("bgt", [P, 3, KD], F32)
        fcw = sb("fcw", [P, NF, 3], F32)
        fcb = sb("fcb", [P, NF], F32)
        wa_bd = sb("wa_bd", [P, 8, P], BF16)
        wx_bd = sb("wx_bd", [P, 8, P], BF16)
        wstage = sb("wstage", [P, 8, P], F32)
        ohm = sb("ohm", [P, NCORES], F32)
        pdm = sb("pdm", [P, NCORES], F32)
        pv1 = sb("pv1", [P, NCORES], F32)
        small = sb("small", [P, 64], F32)
        hst = [sb("h_p", [P, 8], F32), sb("h_s", [P, 8], F32)]
        Sst = [sb("S_p", [P, 8, P], F32), sb("S_s", [P, 8, P], F32)]
        czr = [sb("czr_p", [P, 8, 3], F32), sb("czr_s", [P, 8, 3], F32)]
        cuf = [sb("cu_p", [P, NF, 2], F32), sb("cu_s", [P, NF, 2], F32)]
        rsum = sb("rsum", [P, 8], F32)
        gsum = sb("gsum", [P, 8], F32)
        smb = sb("smb", [P, P], BF16)
        hsm = sb("hsm", [P, 40], F32)
        ycT = sb("ycT", [P, TB], F32)
        tokw = sb("tokw", [64, 2, P], BF16)
        amt = sb("amt", [64, 64], BF16)
        dbgf = sb("dbgf", [P, 512], F32) if debug else None
        ps = TT(es.enter_context(nc.psum_tensor("ps", [P, 6, 512], F32)), "ps")
        psb = TT(es.enter_context(nc.psum_tensor("psb", [P, 2, 1024], BF16)), "psb")
        PB = [Buf("pb%d" % i) for i in range(6)]
        PBB = [Buf("pbb%d" % i) for i in range(2)]

        def slot(i, n=SLOT, dt=F32):
            v = A3.t[:, i * SLOT:(i + 1) * SLOT]
            if dt is BF16:
                v = v.bitcast(BF16)
                return v[:, 0:n]
            return v[:, 0:n]

        TB_ = [Buf("tmp%d" % i) for i in range(NSLOT)]
        ytok = A3.t[:, 0:5 * D].rearrange("p (t d) -> p t d", t=5)
        YB = Buf("ytok")

        yA = A2.t[:, 0:8 * TB].rearrange("p (c t) -> p c t", c=8)
        yB = A2.t[:, 8 * TB:16 * TB].rearrange("p (c t) -> p c t", c=8)
        yC = A2.t[:, 16 * TB:24 * TB].rearrange("p (c t) -> p c t", c=8)
        mT = A2.t[:, 24 * TB:40 * TB].rearrange("p (c t) -> p c t", c=16)
        actT = A2.t[:, 0:NF * TB].rearrange("p (c t) -> p c t", c=NF)
        YAB = [[Buf("yA%d" % i) for i in range(8)], [Buf("yB%d" % i) for i in range(8)], [Buf("yC%d" % i) for i in range(8)]]
        MTB = [Buf("mT%d" % i) for i in range(16)]
        ACTB = [Buf("act%d" % i) for i in range(NF)]
        DR = Buf("dram_misc")
        X1B = [Buf("x1d%d" % i) for i in range(NBLK)]

        dbg_list = []

        def dbg(name, view, R):
            if not debug or (isinstance(debug, (list, tuple, set)) and name not in debug):
                return
            shp = list(view.shape)
            d = nc.dram_tensor("dbg_" + name, shp, F32, kind="ExternalOutput").ap()
            if view.dtype != F32:
                npart = shp[0]
                n = 1
                for q_ in shp[1:]:
                    n *= q_
                sc = dbgf.t[0:npart, 0:n]
                if len(shp) == 3:
                    sc = sc.rearrange("p (a b) -> p a b", a=shp[1])
                cp("dve", sc, view, R, [dbgf])
                dma("sp", lambda e: e.dma_start(out=d, in_=sc), [dbgf], [DR])
            else:
                dma("sp", lambda e: e.dma_start(out=d, in_=view), R, [DR])
            dbg_list.append(name)

        def bl(items):
            out = []
            for it in items:
                if it is None:
                    continue
                out.append(it.b if isinstance(it, TT) else it)
            return out

        def op(eng, fn, R, W):
            return S.op(eng, fn, bl(R), bl(W))

        def dma(eng, fn, R, W, nonc=False):
            if nonc:
                def f2(e, fn=fn):
                    with nc.allow_non_contiguous_dma(reason="small strided parameter/state layout"):
                        return fn(e)
                return S.dma(eng, f2, bl(R), bl(W))
            return S.dma(eng, fn, bl(R), bl(W))

        def mm(out, lhsT, rhs, start, stop, R, W):
            return op("pe", lambda e: e.matmul(out, lhsT=lhsT, rhs=rhs, start=start, stop=stop), R, W)

        def tr(out, in_, idt, R, W):
            return op("pe", lambda e: e.transpose(out=out, in_=in_, identity=idt), R, W)

        def act(out, in_, func, R, W, **kw):
            return op("act", lambda e: e.activation(out=out, in_=in_, func=func, **kw), R, W)

        def ts(eng, out, in0, s1, s2, op0, op1, R, W):
            if s2 is None:
                return op(eng, lambda e: e.tensor_scalar(out=out, in0=in0, scalar1=s1, scalar2=None, op0=op0), R, W)
            return op(eng, lambda e: e.tensor_scalar(out=out, in0=in0, scalar1=s1, scalar2=s2, op0=op0, op1=op1), R, W)

        def tt(eng, out, in0, in1, o, R, W):
            return op(eng, lambda e: e.tensor_tensor(out=out, in0=in0, in1=in1, op=o), R, W)

        def stt(eng, out, in0, sc, in1, op0, op1, R, W):
            return op(eng, lambda e: e.scalar_tensor_tensor(out=out, in0=in0, scalar=sc, in1=in1, op0=op0, op1=op1), R, W)

        def cp(eng, out, in_, R, W):
            if eng == "act":
                return op(eng, lambda e: e.activation(out=out, in_=in_, func=AF.Copy), R, W)
            return op(eng, lambda e: e.tensor_copy(out=out, in_=in_), R, W)

        def pair(i):
            return ps.t[:, 2 * i:2 * i + 2, 0:HALF], [PB[2 * i], PB[2 * i + 1]]

        def fm2(ap):
            return ap.rearrange("p (a b) -> p a b", a=2)

        CST = Buf("consts")
        op("pool", lambda e: e.memset(identf.t[:], 0.0), [], [identf])
        op("pool", lambda e: e.affine_select(out=identf.t[:], in_=identf.t[:], pattern=[[-1, P]], compare_op=ALU.not_equal,
                                             fill=1.0, base=0, channel_multiplier=1), [identf], [identf])
        cp("pool", ident.t[:], identf.t[:], [identf], [ident])
        op("pool", lambda e: e.memset(ones_bf.t[:], 1.0), [], [ones_bf])
        op("pool", lambda e: e.memset(triT.t[:], 1.0), [], [triT])
        op("pool", lambda e: e.affine_select(out=triT.t[:], in_=triT.t[:], pattern=[[1, 64]], compare_op=ALU.is_ge,
                                             fill=0.0, base=0, channel_multiplier=-1), [triT], [triT])
        op("pool", lambda e: e.memset(rstm.t[:], 1.0), [], [rstm])
        for (c0, ln, ch) in SEGS:
            for k in range(ln // ch):
                op("pool", lambda e, c=c0 + k * ch: e.memset(rstm.t[:, c:c + 1], 0.0), [rstm], [rstm])

        def ld_fm(dst, src, nchunk):
            dma("sp", lambda e: e.dma_start(out=dst, in_=src.rearrange("o (c p) -> p (o c)", p=P)), [], [CST], nonc=True)


        def ld_fm3(dst, src, buf, wbufs):
            J = src.shape[0]
            for j in range(J):
                dma("sp", lambda e, j=j: e.dma_start(out=dst[:, :, j], in_=src[j:j + 1, :].rearrange("o (c p) -> p (o c)", p=P)), buf, wbufs, nonc=True)

        def st_fm3(dst, src, rbufs):
            J = dst.shape[0]
            for j in range(J):
                dma("sp", lambda e, j=j: e.dma_start(out=dst[j:j + 1, :].rearrange("o (c p) -> p (o c)", p=P), in_=src[:, :, j]), rbufs, [DR], nonc=True)

        ld_fm(g_pre.t[:], pre_mix_norm, KD)
        ld_fm(g_ffn.t[:], pre_ffn_norm, KD)
        ld_fm(g_mem.t[:], mem_norm, KD)
        ld_fm(cb.t[:], rnn_conv_b, 8)
        ld_fm(ba.t[:], lru_ba, 8)
        ld_fm(bx.t[:], lru_bx, 8)
        ld_fm(lamc.t[:], lru_lambda, 8)
        ld_fm(fcb.t[:], ffn_conv_b, NF)
        ld_fm3(cw.t, rnn_conv_w, [], [CST])
        ld_fm3(fcw.t, ffn_conv_w, [], [CST])
        for l in range(2):
            dma("sp", lambda e, l=l: e.dma_start(out=lb2.t[:, l, :], in_=hg_lb[l:l + 1, :].rearrange("o (c p) -> p (o c)", p=P)), [], [CST], nonc=True)
        dma("sp", lambda e: e.dma_start(out=hgn.t[:], in_=hg_norm.rearrange("o p -> p o")), [], [CST], nonc=True)
        for n_ in range(3):
            dma("sp", lambda e, n_=n_: e.dma_start(out=bgt.t[:, n_, :], in_=b_gate[n_:n_ + 1, :].rearrange("o (c p) -> p (o c)", p=P)), [], [CST], nonc=True)
        dma("sp", lambda e: e.dma_start(out=gpm.t[:], in_=post_mix_norm.partition_broadcast(P)), [], [CST])
        dma("sp", lambda e: e.dma_start(out=gpf.t[:], in_=post_ffn_norm.partition_broadcast(P)), [], [CST])
        dma("sp", lambda e: e.dma_start(out=ohm.t[:], in_=onehot), [], [CST])
        dma("sp", lambda e: e.dma_start(out=pdm.t[:], in_=predm), [], [CST])
        dma("sp", lambda e: e.dma_start(out=pv1.t[:], in_=prev1), [], [CST])
        for (wsrc, wdst) in ((lru_wa, wa_bd), (lru_wx, wx_bd)):
            op("pool", lambda e: e.memset(wstage.t[:], 0.0), [CST], [wstage])
            wv = wsrc.rearrange("(c two) i j -> two i c j", two=2)
            dma("sp", lambda e, wv=wv: e.dma_start(out=wstage.t[0:64, :, 0:64], in_=wv[0]), [], [wstage], nonc=True)
            dma("sp", lambda e, wv=wv: e.dma_start(out=wstage.t[64:128, :, 64:128], in_=wv[1]), [], [wstage], nonc=True)
            cp("pool", wdst.t[:], wstage.t[:], [wstage], [wdst])
        S.barrier()
        act(small.t[:, 0:8], lamc.t[:], AF.Exp, [CST], [small], scale=-1.0)
        act(small.t[:, 0:8], small.t[:, 0:8], AF.Ln, [small], [small], bias=1.0)
        ts("dve", lamc.t[:], small.t[:, 0:8], -8.0, None, ALU.mult, None, [small], [lamc])
        ts("dve", lamc2.t[:], small.t[:, 0:8], -16.0, None, ALU.mult, None, [small], [lamc2])
        tt("dve", small.t[:, 8:16], lb2.t[:, 0, :], lb2.t[:, 1, :], ALU.subtract, [CST], [small])
        act(lbt.t[:], small.t[:, 8:16], AF.Sigmoid, [small], [lbt])
        ts("dve", omlb.t[:], lbt.t[:], -1.0, 1.0, ALU.mult, ALU.add, [lbt], [omlb])
        S.barrier()

        slab_ctr = [0]
        SLQ = [Buf('slq0'), Buf('slq1')]

        def load_slab(src_fn_list):
            sl = slabs[slab_ctr[0] % 3]
            slab_ctr[0] += 1
            for (src, kc, ncols, off, width) in src_fn_list:
                dst = sl.t[:, 0:kc * width].rearrange("p (k n) -> p k n", k=kc)[:, :, off:off + ncols]
                gate = SLQ[slab_ctr[0] % 2]
                dma("pool", lambda e, dst=dst, src=src: e.dma_start(out=dst, in_=src.rearrange("(k p) n -> p k n", p=P)), [], [sl, gate])
            return sl

        pair_ctr = [0]

        def next_pair():
            i = pair_ctr[0] % 3
            pair_ctr[0] += 1
            return pair(i)

        def linear_chunk(sl, kc, width, off, inT_fn, inR, col_splits=NS):
            pv, pbufs = next_pair()
            wv = sl.t[:, 0:kc * width].rearrange("p (k n) -> p k n", k=kc)
            for si, (a, b) in enumerate(col_splits):
                for k in range(kc):
                    mm(pv[:, si, 0:b - a], wv[:, k, off:off + P], inT_fn(k, a, b), k == 0, k == kc - 1,
                       [sl] + inR, [pbufs[si]])
            return pv, pbufs

        XTB = xT.b

        def prenorm(b, src_fn, gain, dstT, dstbuf, srcR):
            for t5 in range(5):
                rows = TILE_ROWS[t5]
                c0 = TILE_COL0[t5]
                dma("sp", lambda e, t5=t5, rows=rows: e.dma_start(out=xt.t[0:rows, :], in_=src_fn(t5)), srcR, [xt])
                act(xsb.t[0:rows, :], xt.t[0:rows, :], AF.Square, [xt], [xsb, small], accum_out=small.t[0:rows, 16:17])
                act(small.t[0:rows, 17:18], small.t[0:rows, 16:17], AF.Sqrt, [small], [small], scale=1.0 / D, bias=EPS)
                op("dve", lambda e, rows=rows: e.reciprocal(out=small.t[0:rows, 17:18], in_=small.t[0:rows, 17:18]), [small], [small])
                ts("dve", xsb.t[0:rows, :], xt.t[0:rows, :], small.t[0:rows, 17:18], None, ALU.mult, None, [xt, small], [xsb])
                for g2 in range(2):
                    pvb = psb.t[:, g2, :].rearrange("p (j r) -> p j r", j=8)
                    for j in range(8):
                        kc = g2 * 8 + j
                        tr(pvb[:, j, 0:rows], xsb.t[0:rows, kc * P:(kc + 1) * P], ident.t[0:rows, 0:rows], [xsb, ident], [PBB[g2]])
                    gv = gain.t[:, g2 * 8:(g2 + 1) * 8].unsqueeze(2).broadcast_to([P, 8, rows])
                    tt("dve", dstT[:, g2 * 8:(g2 + 1) * 8, c0:c0 + rows], pvb[:, :, 0:rows], gv, ALU.mult, [PBB[g2], CST], [dstbuf])

        def x_src(b):
            def f(t5):
                if t5 < 4:
                    return xp[b * PT + t5 * P: b * PT + (t5 + 1) * P, :]
                r0 = (b // 2) * 32 + (b % 2) * ST
                return xs[r0:r0 + ST, :]
            return f

        def x1_src(b):
            def f(t5):
                r0 = b * TB + TILE_COL0[t5]
                return x1d[r0:r0 + TILE_ROWS[t5], :]
            return f

        def build_kv_from_rows(get_k_rows, get_v_rows, kT, vv):
            for mt in range(2):
                kr, kR = get_k_rows(mt)
                vr, vR = get_v_rows(mt)
                cp("dve", vv.t[:, mt, :], vr, vR, [vv])
                for g in range(2):
                    bank = 4 + g
                    pvv = ps.t[:, bank, :].rearrange("p (j r) -> p j r", j=4)
                    for j in range(4):
                        c = g * 4 + j
                        tr(pvv[:, j, :], kr[:, c * P:(c + 1) * P], identf.t[:], kR + [identf], [PB[bank]])
                    act(kT.t[:, g * 4:(g + 1) * 4, mt * P:(mt + 1) * P], pvv, AF.Copy, [PB[bank]], [kT])

        memT = xT.t[:, :, 0:256]
        for mt in range(2):
            dma("sp", lambda e, mt=mt: e.dma_start(out=xt.t[:], in_=mem[mt * P:(mt + 1) * P, :]), [], [xt])
            act(xsb.t[:], xt.t[:], AF.Square, [xt], [xsb, small], accum_out=small.t[:, 16:17])
            act(small.t[:, 17:18], small.t[:, 16:17], AF.Sqrt, [small], [small], scale=1.0 / D, bias=EPS)
            op("dve", lambda e: e.reciprocal(out=small.t[:, 17:18], in_=small.t[:, 17:18]), [small], [small])
            ts("dve", xsb.t[:], xt.t[:], small.t[:, 17:18], None, ALU.mult, None, [xt, small], [xsb])
            for g2 in range(2):
                pvb = psb.t[:, g2, :].rearrange("p (j r) -> p j r", j=8)
                for j in range(8):
                    kc = g2 * 8 + j
                    tr(pvb[:, j, :], xsb.t[:, kc * P:(kc + 1) * P], ident.t[:], [xsb, ident], [PBB[g2]])
                gv = g_mem.t[:, g2 * 8:(g2 + 1) * 8].unsqueeze(2).broadcast_to([P, 8, P])
                tt("dve", memT[:, g2 * 8:(g2 + 1) * 8, mt * P:(mt + 1) * P], pvb, gv, ALU.mult, [PBB[g2], CST], [xT])
        kvrow = A3.t[:, 0:2 * 2048].rearrange("p (m d) -> p m d", m=2)
        KVB = Buf("kvrow")
        for cbk in range(8):
            sl = load_slab([(w_mem_kv[:, cbk * 256:(cbk + 1) * 256], KD, 256, 0, 256)])
            wv = sl.t[:, 0:KD * 256].rearrange("p (k n) -> p k n", k=KD)
            for mt in range(2):
                bank = (cbk * 2 + mt) % 4
                for k in range(KD):
                    mm(ps.t[:, bank, 0:256], memT[:, k, mt * P:(mt + 1) * P], wv[:, k, :], k == 0, k == KD - 1, [sl, xT], [PB[bank]])
                act(kvrow[:, mt, cbk * 256:(cbk + 1) * 256], ps.t[:, bank, 0:256], AF.Copy, [PB[bank]], [KVB])
        for mt in range(2):
            dma("sp", lambda e, mt=mt: e.dma_start(out=mk_o[mt * P:(mt + 1) * P, :], in_=kvrow[:, mt, 0:1024]), [KVB], [DR])
            dma("sp", lambda e, mt=mt: e.dma_start(out=mv_o[mt * P:(mt + 1) * P, :], in_=kvrow[:, mt, 1024:2048]), [KVB], [DR])
        build_kv_from_rows(lambda mt: (kvrow[:, mt, 0:1024], [KVB]), lambda mt: (kvrow[:, mt, 1024:2048], [KVB]), kT_p, v_p)
        S.barrier()

        def load_sample_kv(seq):
            ckr = A3.t[:, 0:2 * 1024].rearrange("p (m d) -> p m d", m=2)
            cvr = A3.t[:, 2048:2048 + 2 * 1024].rearrange("p (m d) -> p m d", m=2)
            dma("sp", lambda e: e.dma_start(out=ckr, in_=ck[seq].rearrange("(m p) d -> p m d", p=P)), [], [KVB])
            dma("sp", lambda e: e.dma_start(out=cvr, in_=cv[seq].rearrange("(m p) d -> p m d", p=P)), [], [KVB])
            build_kv_from_rows(lambda mt: (ckr[:, mt, :], [KVB]), lambda mt: (cvr[:, mt, :], [KVB]), kT_s, v_s)
            S.barrier()

        def zero_states(which):
            op("pool", lambda e: e.memset(hst[which].t[:], 0.0), [], [hst[which]])
            op("pool", lambda e: e.memset(Sst[which].t[:], 0.0), [], [Sst[which]])

        def load_sample_state(seq):
            dma("sp", lambda e: e.dma_start(out=hst[1].t[:], in_=st_h[seq:seq + 1, :].rearrange("o (c p) -> p (o c)", p=P)), [], [hst[1]], nonc=True)
            ld_fm3(czr[1].t, st_conv[seq], [], [czr[1]])
            dma("sp", lambda e: e.dma_start(out=Sst[1].t[:], in_=st_hg[seq].rearrange("h k v -> k h v")), [], [Sst[1]])
            ld_fm3(cuf[1].t, st_fconv[seq], [], [cuf[1]])

        def store_sample_state_mix(seq):
            dma("sp", lambda e: e.dma_start(out=h_s_o[seq:seq + 1, :].rearrange("o (c p) -> p (o c)", p=P), in_=hst[1].t[:]), [hst[1]], [DR], nonc=True)
            st_fm3(conv_s_o[seq], czr[1].t, [czr[1]])
            dma("sp", lambda e: e.dma_start(out=hg_s_o[seq].rearrange("h k v -> k h v"), in_=Sst[1].t[:]), [Sst[1]], [DR])

        def rnn_branch(b, summary, halo_first):
            segs = SEGS[:1] if summary else SEGS
            ZOFF = (0, 3 + PT)
            for pr in range(4):
                sl = load_slab([(w_in[:, pr * 256:(pr + 1) * 256], KD, 256, 0, 256)])
                for q in range(2):
                    ch = pr * 2 + q
                    zt = slot(0, 3 + PT + 3 + ST)
                    pv, pbufs = linear_chunk(sl, KD, 256, q * P, lambda k, a, b_: xT.t[:, k, a:b_], [xT])
                    act(zt[:, 3:3 + HALF], pv[:, 0, :], AF.Copy, [pbufs[0]], [TB_[0]])
                    act(zt[:, 3 + HALF:3 + PT], pv[:, 1, 0:PT - HALF], AF.Copy, [pbufs[1]], [TB_[0]])
                    if not summary:
                        act(zt[:, 3 + PT + 3:3 + PT + 3 + ST], pv[:, 1, PT - HALF:HALF], AF.Copy, [pbufs[1]], [TB_[0]])
                    if halo_first:
                        wv = sl.t[:, 0:KD * 256].rearrange("p (k n) -> p k n", k=KD)
                        bank = 4
                        for k in range(KD):
                            mm(ps.t[:, bank, 0:3], wv[:, k, q * P:(q + 1) * P], xhT[:, k, 0:3], k == 0, k == KD - 1, [sl, XH], [PB[bank]])
                        act(zt[:, 0:3], ps.t[:, bank, 0:3], AF.Copy, [PB[bank]], [TB_[0]])
                    else:
                        cp("act", zt[:, 0:3], czr[0].t[:, ch, :], [czr[0]], [TB_[0]])
                    if not summary:
                        cp("act", zt[:, 3 + PT:3 + PT + 3], czr[1].t[:, ch, :], [czr[1]], [TB_[0]])
                    xr = slot(1, TB)
                    for si, (c0, ln, _) in enumerate(segs):
                        z0 = ZOFF[si]
                        ts("dve", xr[:, c0:c0 + ln], zt[:, z0:z0 + ln], cw.t[:, ch, 0:1], cb.t[:, ch:ch + 1], ALU.mult, ALU.add, [TB_[0], CST], [TB_[1]])
                        for j in range(1, 4):
                            stt("dve", xr[:, c0:c0 + ln], zt[:, z0 + j:z0 + j + ln], cw.t[:, ch, j:j + 1], xr[:, c0:c0 + ln], ALU.mult, ALU.add, [TB_[0], TB_[1], CST], [TB_[1]])
                    cp("act", czr[0].t[:, ch, :], zt[:, PT:PT + 3], [TB_[0]], [czr[0]])
                    if not summary:
                        cp("act", czr[1].t[:, ch, :], zt[:, 3 + PT + ST:3 + PT + ST + 3], [TB_[0]], [czr[1]])
                    ncols = PT if summary else TB
                    xrb = slot(2, TB, BF16)
                    act(xrb[:, 0:ncols], xr[:, 0:ncols], AF.Copy, [TB_[1]], [TB_[2]])
                    splits = ((0, 256), (256, 512)) if summary else NS
                    rr = slot(3, TB)
                    ig = slot(4, TB)
                    for (wbd, bias_t, dstv, dbuf) in ((wa_bd, ba, rr, TB_[3]), (wx_bd, bx, ig, TB_[4])):
                        pv2, pb2 = next_pair()
                        for si, (a, b_) in enumerate(splits):
                            mm(pv2[:, si, 0:b_ - a], wbd.t[:, ch, :], xrb[:, a:b_], True, True, [wbd, TB_[2]], [pb2[si]])
                            act(dstv[:, a:b_], pv2[:, si, 0:b_ - a], AF.Sigmoid, [pb2[si], CST], [dbuf], bias=bias_t.t[:, ch:ch + 1])
                    aa = slot(5, TB)
                    s2 = slot(6, TB)
                    act(aa[:, 0:ncols], rr[:, 0:ncols], AF.Exp, [TB_[3], lamc], [TB_[5]], scale=lamc.t[:, ch:ch + 1])
                    act(s2[:, 0:ncols], rr[:, 0:ncols], AF.Exp, [TB_[3], lamc2], [TB_[6]], scale=lamc2.t[:, ch:ch + 1])
                    act(s2[:, 0:ncols], s2[:, 0:ncols], AF.Sqrt, [TB_[6]], [TB_[6]], scale=-1.0, bias=1.0)
                    tt("dve", ig[:, 0:ncols], ig[:, 0:ncols], xr[:, 0:ncols], ALU.mult, [TB_[4], TB_[1]], [TB_[4]])
                    tt("dve", ig[:, 0:ncols], ig[:, 0:ncols], s2[:, 0:ncols], ALU.mult, [TB_[4], TB_[6]], [TB_[4]])
                    hh = slot(7, TB)
                    for si, (c0, ln, _) in enumerate(segs):
                        op("dve", lambda e, c0=c0, ln=ln, si=si, ch=ch, hh=hh, aa=aa, ig=ig: e.tensor_tensor_scan(
                            out=hh[:, c0:c0 + ln], data0=aa[:, c0:c0 + ln], data1=ig[:, c0:c0 + ln],
                            initial=hst[si].t[:, ch:ch + 1], op0=ALU.mult, op1=ALU.add), [TB_[5], TB_[4], hst[si]], [TB_[7]])
                        cp("dve", hst[si].t[:, ch:ch + 1], hh[:, c0 + ln - 1:c0 + ln], [TB_[7]], [hst[si]])
                    if summary:
                        op("dve", lambda e, ch=ch, rr=rr: e.reduce_sum(out=small.t[:, 20:21], in_=rr[:, 0:PT], axis=AX.X), [TB_[3]], [small])
                        tt("dve", rsum.t[:, ch:ch + 1], rsum.t[:, ch:ch + 1], small.t[:, 20:21], ALU.add, [small, rsum], [rsum])
                    else:
                        act(yA[:, ch, :], hh[:, 0:TB], AF.Copy, [TB_[7]], [YAB[0][ch]])

        def hg_branch(b, summary):
            segs = SEGS[:1] if summary else SEGS
            ncols = TB
            splits = NS
            for hp in range(4):
                groups = (("f", 2048), ("i", 3072)) if summary else (("q", 1024), ("f", 2048), ("i", 3072), ("o", 4096))
                stage = {}
                for (nm, cbase) in groups:
                    sl = load_slab([(w_in[:, cbase + hp * 256: cbase + (hp + 1) * 256], KD, 256, 0, 256)])
                    for q in range(2):
                        pv, pbufs = linear_chunk(sl, KD, 256, q * P, lambda k, a, b_: xT.t[:, k, a:b_], [xT], col_splits=splits)
                        if nm == "q":
                            si_ = 0 + q
                            dst = slot(si_, TB)
                            fn_ = AF.Copy
                        elif nm == "f":
                            si_ = 2 + q
                            dst = slot(si_, TB)
                            fn_ = AF.Sigmoid
                        elif nm == "o":
                            si_ = 4 + q
                            dst = slot(si_, TB)
                            fn_ = AF.Sigmoid
                        else:
                            si_ = 6 + q
                            dst = slot(si_, TB, BF16)
                            fn_ = AF.Copy
                        for si, (a, b_) in enumerate(splits):
                            act(dst[:, a:b_], pv[:, si, 0:b_ - a], fn_, [pbufs[si]], [TB_[si_]])
                        stage[(nm, q)] = (dst, TB_[si_])
                for q in range(2):
                    h = hp * 2 + q
                    ff, ffb = stage[("f", q)]
                    vT, vTb = stage[("i", q)]
                    ts("dve", ff[:, 0:ncols], ff[:, 0:ncols], omlb.t[:, h:h + 1], lbt.t[:, h:h + 1], ALU.mult, ALU.add, [ffb, omlb, lbt], [ffb])
                    kk = slot(8, TB)
                    ts("dve", kk[:, 0:ncols], ff[:, 0:ncols], -1.0, 1.0, ALU.mult, ALU.add, [ffb], [TB_[8]])
                    act(ff[:, 0:ncols], ff[:, 0:ncols], AF.Ln, [ffb], [ffb])
                    D0 = summary and b == 0 and h == 0
                    if D0:
                        dbg("g", ff[:, 0:PT], [ffb])
                        dbg("kk", kk[:, 0:PT], [TB_[8]])
                        dbg("vT", vT[:, 0:PT], [vTb])
                    bc = slot(9, TB)
                    op("dve", lambda e, bc=bc, ff=ff, ncols=ncols: e.tensor_tensor_scan(
                        out=bc[:, 0:ncols], data0=rstm.t[:, 0:ncols], data1=ff[:, 0:ncols],
                        initial=0.0, op0=ALU.mult, op1=ALU.add), [rstm, ffb], [TB_[9]])
                    chunks = []
                    for si, (c0, ln, chn) in enumerate(segs):
                        for kx in range(ln // chn):
                            chunks.append((si, c0 + kx * chn, chn, len(chunks) if si == 0 else 8))
                    NPC = PT // 64
                    bcp = bc[:, 0:PT].rearrange("p (c t) -> p c t", c=NPC)
                    cp("dve", hsm.t[:, 0:NPC], bcp[:, :, 31], [TB_[9]], [hsm])
                    cp("dve", hsm.t[:, 8:9], bc[:, PT + ST // 2 - 1:PT + ST // 2], [TB_[9]], [hsm])
                    cp("dve", hsm.t[:, 18:18 + NPC], bcp[:, :, 63], [TB_[9]], [hsm])
                    cp("dve", hsm.t[:, 26:27], bc[:, TB - 1:TB], [TB_[9]], [hsm])
                    act(small.t[:, 24:33], hsm.t[:, 0:9], AF.Exp, [hsm], [small])
                    act(small.t[:, 33:42], hsm.t[:, 18:27], AF.Exp, [hsm], [small])
                    tt("dve", hsm.t[:, 27:36], hsm.t[:, 18:27], hsm.t[:, 0:9], ALU.subtract, [hsm], [hsm])
                    act(small.t[:, 42:51], hsm.t[:, 27:36], AF.Exp, [hsm], [small])
                    if summary:
                        op("dve", lambda e: e.reduce_sum(out=small.t[:, 21:22], in_=hsm.t[:, 18:18 + 8], axis=AX.X), [hsm], [small])
                        tt("dve", gsum.t[:, h:h + 1], gsum.t[:, h:h + 1], small.t[:, 21:22], ALU.add, [small, gsum], [gsum])
                    tt("dve", bcp, bcp, hsm.t[:, 0:NPC].unsqueeze(2).broadcast_to([P, NPC, 64]), ALU.subtract, [TB_[9], hsm], [TB_[9]])
                    ts("dve", bc[:, PT:TB], bc[:, PT:TB], hsm.t[:, 8:9], None, ALU.subtract, None, [TB_[9], hsm], [TB_[9]])
                    e2 = slot(10, TB)
                    act(e2[:, 0:TB], bc[:, 0:TB], AF.Exp, [TB_[9]], [TB_[10]], scale=-1.0)
                    ke = slot(11, TB, BF16)
                    tt("dve", ke[:, 0:TB], kk[:, 0:TB], e2[:, 0:TB], ALU.mult, [TB_[8], TB_[10]], [TB_[11]])
                    kd = slot(12, TB, BF16)
                    tt("dve", kd[:, 0:PT].rearrange("p (c t) -> p c t", c=NPC), ke[:, 0:PT].rearrange("p (c t) -> p c t", c=NPC),
                       small.t[:, 42:42 + NPC].unsqueeze(2).broadcast_to([P, NPC, 64]), ALU.mult, [TB_[11], small], [TB_[12]])
                    ts("dve", kd[:, PT:TB], ke[:, PT:TB], small.t[:, 50:51], None, ALU.mult, None, [TB_[11], small], [TB_[12]])
                    if D0:
                        dbg("bc", bc[:, 0:PT], [TB_[9]])
                        dbg("hsm", hsm.t[:], [hsm])
                        dbg("small", small.t[:], [small])
                        dbg("e2", e2[:, 0:PT], [TB_[10]])
                        dbg("ke", ke[:, 0:PT], [TB_[11]])
                        dbg("kd", kd[:, 0:PT], [TB_[12]])
                    if not summary:
                        zq, zqb = stage[("q", q)]
                        og, ogb = stage[("o", q)]
                        e1 = slot(10, TB)
                        act(e1[:, 0:TB], bc[:, 0:TB], AF.Exp, [TB_[9]], [TB_[10]])
                        qe = slot(13, TB, BF16)
                        tt("dve", qe[:, 0:TB], zq[:, 0:TB], e1[:, 0:TB], ALU.mult, [zqb, TB_[10]], [TB_[13]])
                        oo = slot(14, TB)
                    for (si, cc0, chn, ci) in chunks:
                        Sx = Sst[si]
                        pvb = psb.t[0:chn, 0, 0:2 * P].rearrange("p (j r) -> p j r", j=2)
                        tr(pvb[:, 0, :], kd[:, cc0:cc0 + chn], ident.t[:], [TB_[12], ident], [PBB[0]])
                        tr(pvb[:, 1, :], vT[:, cc0:cc0 + chn], ident.t[:], [vTb, ident], [PBB[0]])
                        cp("dve", tokw.t[0:chn, 0:2, :], pvb, [PBB[0]], [tokw])
                        if not summary:
                            mm(ps.t[0:chn, 4, 0:chn], ke[:, cc0:cc0 + chn], qe[:, cc0:cc0 + chn], True, True, [TB_[11], TB_[13]], [PB[4]])
                            tt("dve", amt.t[0:chn, 0:chn], ps.t[0:chn, 4, 0:chn], triT.t[0:chn, 0:chn], ALU.mult, [PB[4], triT], [amt])
                            act(smb.t[:], Sx.t[:, h, :], AF.Copy, [Sx, small], [smb], scale=small.t[:, 24 + ci:25 + ci])
                            mm(ps.t[:, 5, 0:chn], smb.t[:], qe[:, cc0:cc0 + chn], True, False, [smb, TB_[13]], [PB[5]])
                            mm(ps.t[:, 5, 0:chn], tokw.t[0:chn, 1, :], amt.t[0:chn, 0:chn], False, True, [tokw, amt], [PB[5]])
                            act(oo[:, cc0:cc0 + chn], ps.t[:, 5, 0:chn], AF.Copy, [PB[5]], [TB_[14]])
                        mm(ps.t[:, 3, 0:P], tokw.t[0:chn, 0, :], tokw.t[0:chn, 1, :], True, True, [tokw], [PB[3]])
                        stt("dve", Sx.t[:, h, :], Sx.t[:, h, :], small.t[:, 33 + ci:34 + ci], ps.t[:, 3, 0:P], ALU.mult, ALU.add, [Sx, small, PB[3]], [Sx])
                        if D0 and ci == 0:
                            dbg("tokw", tokw.t[:, 0:2, :], [tokw])
                            dbg("S0", Sx.t[:, 0, :], [Sx])
                    if not summary:
                        osq = slot(15, TB, BF16)
                        act(osq[:, 0:TB], oo[:, 0:TB], AF.Square, [TB_[14]], [TB_[15]])
                        pv3, pb3 = next_pair()
                        rs_ = slot(16, TB)
                        for si, (a, b_) in enumerate(NS):
                            mm(pv3[:, si, :], ones_bf.t[:], osq[:, a:b_], True, True, [ones_bf, TB_[15]], [pb3[si]])
                            act(rs_[:, a:b_], pv3[:, si, :], AF.Sqrt, [pb3[si]], [TB_[16]], scale=1.0 / 128.0, bias=EPS)
                        op("dve", lambda e, rs_=rs_: e.reciprocal(out=rs_[:, 0:TB], in_=rs_[:, 0:TB]), [TB_[16]], [TB_[16]])
                        stt("dve", oo[:, 0:TB], oo[:, 0:TB], hgn.t[:, 0:1], rs_[:, 0:TB], ALU.mult, ALU.mult, [TB_[14], TB_[16], CST], [TB_[14]])
                        tt("dve", yB[:, h, :], oo[:, 0:TB], og[:, 0:TB], ALU.mult, [TB_[14], ogb], [YAB[1][h]])

        def xattn_branch(b):
            for a4 in range(4):
                sl = load_slab([(w_in[:, 5120 + a4 * 256: 5120 + (a4 + 1) * 256], KD, 256, 0, 256)])
                qxs = [slot(0, TB, BF16), slot(1, TB, BF16)]
                for q in range(2):
                    pv, pbufs = linear_chunk(sl, KD, 256, q * P, lambda k, a, b_: xT.t[:, k, a:b_], [xT])
                    act(fm2(qxs[q][:, 0:TB]), pv, AF.Copy, pbufs, [TB_[q]], scale=1.0 / 16.0)
                pT = [slot(2, TB, BF16), slot(3, TB, BF16)]
                for t5 in range(5):
                    rows = TILE_ROWS[t5]
                    c0 = TILE_COL0[t5]
                    kTx, vx = (kT_p, v_p) if t5 < 4 else (kT_s, v_s)
                    bank = 4 + (t5 % 2)
                    for dc in range(2):
                        mm(ps.t[0:rows, bank, 0:256], qxs[dc][:, c0:c0 + rows], kTx.t[:, a4 * 2 + dc, :], dc == 0, dc == 1, [TB_[dc], kTx], [PB[bank]])
                    op("dve", lambda e, rows=rows, bank=bank: e.reduce_max(out=small.t[0:rows, 52:53], in_=ps.t[0:rows, bank, 0:256], axis=AX.X), [PB[bank]], [small])
                    ts("dve", small.t[0:rows, 53:54], small.t[0:rows, 52:53], -1.0, None, ALU.mult, None, [small], [small])
                    pf = slot(4, 256)
                    act(pf[0:rows, :], ps.t[0:rows, bank, 0:256], AF.Exp, [PB[bank], small], [TB_[4], small], bias=small.t[0:rows, 53:54], accum_out=small.t[0:rows, 54:55])
                    op("dve", lambda e, rows=rows: e.reciprocal(out=small.t[0:rows, 55:56], in_=small.t[0:rows, 54:55]), [small], [small])
                    pn = slot(5, 256, BF16)
                    ts("dve", pn[0:rows, :], pf[0:rows, :], small.t[0:rows, 55:56], None, ALU.mult, None, [TB_[4], small], [TB_[5]])
                    pvb = psb.t[:, 1, 0:2 * P].rearrange("p (j r) -> p j r", j=2)
                    for mc in range(2):
                        tr(pvb[:, mc, 0:rows], pn[0:rows, mc * P:(mc + 1) * P], ident.t[0:rows, 0:rows], [TB_[5], ident], [PBB[1]])
                    for mc in range(2):
                        cp("dve", pT[mc][:, c0:c0 + rows], pvb[:, mc, 0:rows], [PBB[1]], [TB_[2 + mc]])
                for dc in range(2):
                    chn = a4 * 2 + dc
                    for (c0, ln, kvv, bank) in ((0, PT, v_p, 2), (PT, ST, v_s, 3)):
                        for mc in range(2):
                            mm(ps.t[:, bank, 0:ln], kvv.t[:, mc, chn * P:(chn + 1) * P], pT[mc][:, c0:c0 + ln], mc == 0, mc == 1, [kvv, TB_[2 + mc]], [PB[bank]])
                        act(yC[:, chn, c0:c0 + ln], ps.t[:, bank, 0:ln], AF.Copy, [PB[bank]], [YAB[2][chn]])

        def merge_gates(b):
            maccs = [slot(0, TB), slot(3, TB)]
            MB = [TB_[0], TB_[3]]
            sg = slot(1, TB)
            tmp = slot(2, TB)
            yv = (yA, yB, yC)
            for jp in range(8):
                for nb in range(3):
                    gsl = load_slab([(w_in[:, 6144 + nb * 2048 + jp * 256: 6144 + nb * 2048 + (jp + 1) * 256], KD, 256, 0, 256)])
                    bsl = load_slab([(w_branch[nb][:, jp * 256:(jp + 1) * 256], 8, 256, 0, 256)])
                    ybr = yv[nb]
                    for q in range(2):
                        j = jp * 2 + q
                        macc = maccs[q]
                        pvg, pbg = linear_chunk(gsl, KD, 256, q * P, lambda k, a, b_: xT.t[:, k, a:b_], [xT])
                        act(fm2(sg[:, 0:TB]), pvg, AF.Sigmoid, pbg + [CST], [TB_[1]], bias=bgt.t[:, nb, j:j + 1])
                        pvp, pbp = linear_chunk(bsl, 8, 256, q * P, lambda k, a, b_, ybr=ybr: ybr[:, k, a:b_], YAB[nb])
                        if nb == 0:
                            tt("dve", fm2(macc[:, 0:TB]), pvp, fm2(sg[:, 0:TB]), ALU.mult, pbp + [TB_[1]], [MB[q]])
                        elif nb == 1:
                            tt("dve", fm2(tmp[:, 0:TB]), pvp, fm2(sg[:, 0:TB]), ALU.mult, pbp + [TB_[1]], [TB_[2]])
                            tt("dve", macc[:, 0:TB], macc[:, 0:TB], tmp[:, 0:TB], ALU.add, [MB[q], TB_[2]], [MB[q]])
                        else:
                            tt("dve", fm2(tmp[:, 0:TB]), pvp, fm2(sg[:, 0:TB]), ALU.mult, pbp + [TB_[1]], [TB_[2]])
                            tt("dve", mT[:, j, :], macc[:, 0:TB], tmp[:, 0:TB], ALU.add, [MB[q], TB_[2]], [MTB[j]])

        def project_tokmajor(wsrc, kc, inT, inBufs):
            for oc in range(16):
                if kc == KD:
                    if oc % 2 == 0:
                        sl = load_slab([(wsrc[:, oc * P:(oc + 2) * P], kc, 256, 0, 256)])
                        project_tokmajor.sl = sl
                    sl = project_tokmajor.sl
                    width, off = 256, (oc % 2) * P
                else:
                    sl = load_slab([(wsrc[:, oc * P:(oc + 1) * P], kc, P, 0, P)])
                    width, off = P, 0
                pv, pbufs = linear_chunk(sl, kc, width, off, lambda k, a, b_: inT[:, k, a:b_], inBufs)
                yc = ycT.t
                act(fm2(yc[:, 0:TB]), pv, AF.Copy, pbufs, [ycT])
                bank = oc % 2
                bnk = 4 + bank
                pvv = ps.t[:, bnk, :].rearrange("p (j r) -> p j r", j=4)
                for t5 in range(4):
                    tr(pvv[:, t5, :], yc[:, t5 * P:(t5 + 1) * P], identf.t[:], [ycT, identf], [PB[bnk]])
                act(ytok[:, 0:4, oc * P:(oc + 1) * P], pvv, AF.Copy, [PB[bnk]], [YB])
                tr(ps.t[0:ST, 3, 0:P], yc[:, PT:PT + ST], identf.t[:], [ycT, identf], [PB[3]])
                act(ytok[0:ST, 4, oc * P:(oc + 1) * P], ps.t[0:ST, 3, 0:P], AF.Copy, [PB[3]], [YB])

        def post_norm_residual(b, gain, res_src, resR, dst_fn, dstW):
            for t5 in range(5):
                rows = TILE_ROWS[t5]
                dma("sp", lambda e, t5=t5, rows=rows: e.dma_start(out=xt.t[0:rows, :], in_=res_src(t5)), resR, [xt])
                act(xsb.t[0:rows, :], ytok[0:rows, t5, :], AF.Square, [YB], [xsb, small], accum_out=small.t[0:rows, 16:17])
                act(small.t[0:rows, 17:18], small.t[0:rows, 16:17], AF.Sqrt, [small], [small], scale=1.0 / D, bias=EPS)
                op("dve", lambda e, rows=rows: e.reciprocal(out=small.t[0:rows, 17:18], in_=small.t[0:rows, 17:18]), [small], [small])
                stt("dve", ytok[0:rows, t5, :], ytok[0:rows, t5, :], small.t[0:rows, 17:18], gain.t[0:rows, :], ALU.mult, ALU.mult, [YB, small, CST], [YB])
                tt("dve", xt.t[0:rows, :], xt.t[0:rows, :], ytok[0:rows, t5, :], ALU.add, [xt, YB], [xt])
                dma("sp", lambda e, t5=t5, rows=rows: e.dma_start(out=dst_fn(t5), in_=xt.t[0:rows, :]), [xt], dstW)

        xhT_t = sb("xhT", [P, KD, 4], BF16)
        xhT = xhT_t.t
        XH = xhT_t.b

        def build_xhalo():
            dma("sp", lambda e: e.dma_start(out=xt.t[0:3, :], in_=xhalo), [], [xt])
            act(xsb.t[0:3, :], xt.t[0:3, :], AF.Square, [xt], [xsb, small], accum_out=small.t[0:3, 16:17])
            act(small.t[0:3, 17:18], small.t[0:3, 16:17], AF.Sqrt, [small], [small], scale=1.0 / D, bias=EPS)
            op("dve", lambda e: e.reciprocal(out=small.t[0:3, 17:18], in_=small.t[0:3, 17:18]), [small], [small])
            ts("dve", xsb.t[0:3, :], xt.t[0:3, :], small.t[0:3, 17:18], None, ALU.mult, None, [xt, small], [xsb])
            for g2 in range(2):
                pvb = psb.t[:, g2, :].rearrange("p (j r) -> p j r", j=8)
                for j in range(8):
                    kc = g2 * 8 + j
                    tr(pvb[:, j, 0:3], xsb.t[0:3, kc * P:(kc + 1) * P], ident.t[0:3, 0:3], [xsb, ident], [PBB[g2]])
                gv = g_pre.t[:, g2 * 8:(g2 + 1) * 8].unsqueeze(2).broadcast_to([P, 8, 3])
                tt("dve", xhT[:, g2 * 8:(g2 + 1) * 8, 0:3], pvb[:, :, 0:3], gv, ALU.mult, [PBB[g2], CST], [XH])

        build_xhalo()
        zero_states(0)
        op("pool", lambda e: e.memset(rsum.t[:], 0.0), [], [rsum])
        op("pool", lambda e: e.memset(gsum.t[:], 0.0), [], [gsum])
        S.barrier()
        for b in range(NBLK):
            prenorm(b, x_src(b), g_pre, xT.t, XTB, [])
            rnn_branch(b, True, b == 0)
            hg_branch(b, True)
            S.barrier()
        sumv = A3.t[:, 0:NCORES * SUMW].rearrange("p (r w) -> p r w", r=NCORES)
        SUMB = Buf("sumv")
        mine = A3.t[:, NCORES * SUMW:NCORES * SUMW + SUMW]
        MINEB = Buf("mine")
        tt("dve", mine[:, 0:8], rsum.t[:], lamc.t[:], ALU.mult, [rsum, lamc], [MINEB])
        act(mine[:, 0:8], mine[:, 0:8], AF.Exp, [MINEB], [MINEB])
        cp("dve", mine[:, 8:16], hst[0].t[:], [hst[0]], [MINEB])
        act(mine[:, 16:24], gsum.t[:], AF.Exp, [gsum], [MINEB])
        cp("dve", mine[:, 24:24 + 1024], Sst[0].t[:].rearrange("p h v -> p (h v)"), [Sst[0]], [MINEB])
        for r in range(NCORES):
            ts("dve", sumv[:, r, :], mine, ohm.t[:, r:r + 1], None, ALU.mult, None, [MINEB, CST], [SUMB])
        AR1 = Buf("ar1")
        dma("pool", lambda e: e.dma_start(out=ar1_in.ap(), in_=A3.t[:, 0:NCORES * SUMW]), [SUMB], [AR1])
        S.cc(lambda e: e.collective_compute("AllReduce", ALU.add, replica_groups=[list(range(NCORES))],
                                            ins=[ar1_in.ap().opt()], outs=[ar1_out.ap().opt()]), [AR1], [AR1])
        dma("pool", lambda e: e.dma_start(out=A3.t[:, 0:NCORES * SUMW], in_=ar1_out.ap()), [AR1], [SUMB])
        zero_states(0)
        for r in range(NCORES):
            m_r = pdm.t[:, r:r + 1]
            ts("dve", small.t[:, 0:8], sumv[:, r, 0:8], -1.0, m_r, ALU.add, ALU.mult, [SUMB, CST], [small])
            ts("dve", small.t[:, 0:8], small.t[:, 0:8], 1.0, None, ALU.add, None, [small], [small])
            ts("dve", small.t[:, 8:16], sumv[:, r, 16:24], -1.0, m_r, ALU.add, ALU.mult, [SUMB, CST], [small])
            ts("dve", small.t[:, 8:16], small.t[:, 8:16], 1.0, None, ALU.add, None, [small], [small])
            tt("dve", hst[0].t[:], hst[0].t[:], small.t[:, 0:8], ALU.mult, [hst[0], small], [hst[0]])
            stt("dve", hst[0].t[:], sumv[:, r, 8:16], m_r, hst[0].t[:], ALU.mult, ALU.add, [SUMB, CST, hst[0]], [hst[0]])
            tt("dve", Sst[0].t[:], Sst[0].t[:], small.t[:, 8:16].unsqueeze(2).broadcast_to([P, 8, P]), ALU.mult, [Sst[0], small], [Sst[0]])
            stt("dve", Sst[0].t[:], sumv[:, r, 24:24 + 1024].rearrange("p (h v) -> p h v", h=8), m_r, Sst[0].t[:], ALU.mult, ALU.add,
                [SUMB, CST, Sst[0]], [Sst[0]])
        S.barrier()

        for b in range(NBLK):
            seq = b // 2
            if b % 2 == 0:
                load_sample_state(seq)
                load_sample_kv(seq)
            prenorm(b, x_src(b), g_pre, xT.t, XTB, [])
            rnn_branch(b, False, b == 0)
            S.barrier()
            hg_branch(b, False)
            S.barrier()
            xattn_branch(b)
            S.barrier()
            merge_gates(b)
            S.barrier()
            project_tokmajor(w_out, KD, mT, MTB)
            post_norm_residual(b, gpm, x_src(b), [], lambda t5, b=b: x1d[b * TB + TILE_COL0[t5]: b * TB + TILE_COL0[t5] + TILE_ROWS[t5], :], [X1B[b]])
            S.barrier()
            if b % 2 == 1:
                store_sample_state_mix(seq)
        dma("sp", lambda e: e.dma_start(out=h_p_o.rearrange("o (c p) -> p (o c)", p=P), in_=hst[0].t[:]), [hst[0]], [DR], nonc=True)
        st_fm3(conv_p_o, czr[0].t, [czr[0]])
        dma("sp", lambda e: e.dma_start(out=hg_p_o.rearrange("h k v -> k h v"), in_=Sst[0].t[:]), [Sst[0]], [DR])
        S.barrier()

        def ffn_u_chunk_slab(jp):
            return load_slab([(w_ffn_up[:, jp * 256:(jp + 1) * 256], KD, 256, 0, 256)])

        def last_tile_src(t5):
            r0 = 3 * TB + 384
            return x1d[r0:r0 + P, :]
        dma("sp", lambda e: e.dma_start(out=xt.t[:], in_=last_tile_src(0)), [X1B[3]], [xt])
        act(xsb.t[:], xt.t[:], AF.Square, [xt], [xsb, small], accum_out=small.t[:, 16:17])
        act(small.t[:, 17:18], small.t[:, 16:17], AF.Sqrt, [small], [small], scale=1.0 / D, bias=EPS)
        op("dve", lambda e: e.reciprocal(out=small.t[:, 17:18], in_=small.t[:, 17:18]), [small], [small])
        ts("dve", xsb.t[:], xt.t[:], small.t[:, 17:18], None, ALU.mult, None, [xt, small], [xsb])
        for g2 in range(2):
            pvb = psb.t[:, g2, :].rearrange("p (j r) -> p j r", j=8)
            for j in range(8):
                kc = g2 * 8 + j
                tr(pvb[:, j, :], xsb.t[:, kc * P:(kc + 1) * P], ident.t[:], [xsb, ident], [PBB[g2]])
            gv = g_ffn.t[:, g2 * 8:(g2 + 1) * 8].unsqueeze(2).broadcast_to([P, 8, P])
            tt("dve", xT.t[:, g2 * 8:(g2 + 1) * 8, 0:P], pvb, gv, ALU.mult, [PBB[g2], CST], [xT])
        uh = A3.t[:, NCORES * 88: NCORES * 88 + 88].rearrange("p (c j) -> p c j", c=NF)
        UHB = Buf("uh")
        for jp in range(22):
            sl = ffn_u_chunk_slab(jp)
            wv = sl.t[:, 0:KD * 256].rearrange("p (k n) -> p k n", k=KD)
            for q in range(2):
                j = jp * 2 + q
                bank = j % 2
                for k in range(KD):
                    mm(ps.t[:, bank, 0:2], wv[:, k, q * P:(q + 1) * P], xT.t[:, k, 126:128], k == 0, k == KD - 1, [sl, xT], [PB[bank]])
                act(uh[:, j, :], ps.t[:, bank, 0:2], AF.Copy, [PB[bank]], [UHB])
        uall = A3.t[:, 0:NCORES * 88].rearrange("p (r w) -> p r w", r=NCORES)
        UALL = Buf("uall")
        for r in range(NCORES):
            ts("dve", uall[:, r, :], uh.rearrange("p c j -> p (c j)"), ohm.t[:, r:r + 1], None, ALU.mult, None, [UHB, CST], [UALL])
        AR2 = Buf("ar2")
        dma("pool", lambda e: e.dma_start(out=ar2_in.ap(), in_=A3.t[:, 0:NCORES * 88]), [UALL], [AR2])
        S.cc(lambda e: e.collective_compute("AllReduce", ALU.add, replica_groups=[list(range(NCORES))],
                                            ins=[ar2_in.ap().opt()], outs=[ar2_out.ap().opt()]), [AR2], [AR2])
        dma("pool", lambda e: e.dma_start(out=A3.t[:, 0:NCORES * 88], in_=ar2_out.ap()), [AR2], [UALL])
        cufp = cuf[0].t[:].rearrange("p c j -> p (c j)")
        op("pool", lambda e: e.memset(cuf[0].t[:], 0.0), [], [cuf[0]])
        for r in range(NCORES):
            stt("dve", cufp, uall[:, r, :], pv1.t[:, r:r + 1], cufp, ALU.mult, ALU.add, [UALL, CST, cuf[0]], [cuf[0]])
        S.barrier()

        UOFF = (0, 2 + PT)
        for b in range(NBLK):
            seq = b // 2
            if b % 2 == 0:
                ld_fm3(cuf[1].t, st_fconv[seq], [], [cuf[1]])
            prenorm(b, x1_src(b), g_ffn, xT.t, XTB, [X1B[b]])
            for jp in range(22):
                slu = load_slab([(w_ffn_up[:, jp * 256:(jp + 1) * 256], KD, 256, 0, 256)])
                slv = load_slab([(w_ffn_up[:, FFN + jp * 256: FFN + (jp + 1) * 256], KD, 256, 0, 256)])
                for q in range(2):
                    j = jp * 2 + q
                    ut = slot(0, 2 + PT + 2 + ST)
                    pvu, pbu = linear_chunk(slu, KD, 256, q * P, lambda k, a, b_: xT.t[:, k, a:b_], [xT])
                    act(ut[:, 2:2 + HALF], pvu[:, 0, :], AF.Copy, [pbu[0]], [TB_[0]])
                    act(ut[:, 2 + HALF:2 + PT], pvu[:, 1, 0:PT - HALF], AF.Copy, [pbu[1]], [TB_[0]])
                    act(ut[:, 2 + PT + 2:2 + PT + 2 + ST], pvu[:, 1, PT - HALF:HALF], AF.Copy, [pbu[1]], [TB_[0]])
                    cp("act", ut[:, 0:2], cuf[0].t[:, j, :], [cuf[0]], [TB_[0]])
                    cp("act", ut[:, 2 + PT:2 + PT + 2], cuf[1].t[:, j, :], [cuf[1]], [TB_[0]])
                    pvv_, pbv = linear_chunk(slv, KD, 256, q * P, lambda k, a, b_: xT.t[:, k, a:b_], [xT])
                    uc = slot(1, TB)
                    for si, (c0, ln, _) in enumerate(SEGS):
                        z0 = UOFF[si]
                        ts("dve", uc[:, c0:c0 + ln], ut[:, z0:z0 + ln], fcw.t[:, j, 0:1], fcb.t[:, j:j + 1], ALU.mult, ALU.add, [TB_[0], CST], [TB_[1]])
                        for jj in range(1, 3):
                            stt("dve", uc[:, c0:c0 + ln], ut[:, z0 + jj:z0 + jj + ln], fcw.t[:, j, jj:jj + 1], uc[:, c0:c0 + ln], ALU.mult, ALU.add, [TB_[0], TB_[1], CST], [TB_[1]])
                    cp("act", cuf[0].t[:, j, :], ut[:, PT:PT + 2], [TB_[0]], [cuf[0]])
                    cp("act", cuf[1].t[:, j, :], ut[:, 2 + PT + ST:2 + PT + ST + 2], [TB_[0]], [cuf[1]])
                    act(uc[:, 0:TB], uc[:, 0:TB], AF.Gelu_apprx_tanh, [TB_[1]], [TB_[1]])
                    tt("dve", fm2(actT[:, j, :]), pvv_, fm2(uc[:, 0:TB]), ALU.mult, pbv + [TB_[1]], [ACTB[j]])
            S.barrier()
            if b % 2 == 1:
                st_fm3(fconv_s_o[seq], cuf[1].t, [cuf[1]])
            project_tokmajor(w_ffn_down, NF, actT, ACTB)

            def out_dst(t5, b=b):
                if t5 < 4:
                    return y_p[b * PT + t5 * P: b * PT + (t5 + 1) * P, :]
                r0 = (b // 2) * 32 + (b % 2) * ST
                return y_s[r0:r0 + ST, :]
            post_norm_residual(b, gpf, x1_src(b), [X1B[b]], out_dst, [DR])
            S.barrier()
        st_fm3(fconv_p_o, cuf[0].t, [cuf[0]])

        S.lower(nc)
    return nc


_CACHE = {}


def _get_program():
    if "nc" not in _CACHE:
        _CACHE["nc"] = build_program()
    return _CACHE["nc"]


def kernel(x_prompt, x_sample, cache_mem_k, cache_mem_v, state_rnn_h, state_rnn_conv, state_hg,
           state_ffn_conv, mem_prompt, pre_mix_norm, w_in, rnn_conv_w, rnn_conv_b, lru_wa, lru_ba,
           lru_wx, lru_bx, lru_lambda, hg_lb, hg_norm, mem_norm, w_mem_kv, w_branch, b_gate, w_out,
           post_mix_norm, pre_ffn_norm, w_ffn_up, ffn_conv_w, ffn_conv_b, w_ffn_down, post_ffn_norm):
    nc = _get_program()
    in_maps = make_in_maps(x_prompt, x_sample, cache_mem_k, cache_mem_v, state_rnn_h, state_rnn_conv, state_hg,
                           state_ffn_conv, mem_prompt, pre_mix_norm, w_in, rnn_conv_w, rnn_conv_b, lru_wa, lru_ba,
                           lru_wx, lru_bx, lru_lambda, hg_lb, hg_norm, mem_norm, w_mem_kv, w_branch, b_gate, w_out,
                           post_mix_norm, pre_ffn_norm, w_ffn_up, ffn_conv_w, ffn_conv_b, w_ffn_down, post_ffn_norm)
    res = run_bass_kernel_spmd(nc, in_maps, core_ids=list(range(NCORES)))
    return assemble(res.results)


def make_in_maps(x_prompt, x_sample, cache_mem_k, cache_mem_v, state_rnn_h, state_rnn_conv, state_hg,
                 state_ffn_conv, mem_prompt, pre_mix_norm, w_in, rnn_conv_w, rnn_conv_b, lru_wa, lru_ba,
                 lru_wx, lru_bx, lru_lambda, hg_lb, hg_norm, mem_norm, w_mem_kv, w_branch, b_gate, w_out,
                 post_mix_norm, pre_ffn_norm, w_ffn_up, ffn_conv_w, ffn_conv_b, w_ffn_down, post_ffn_norm):
    f = lambda a: np.ascontiguousarray(np.asarray(a, dtype=np.float32))
    x_prompt = f(x_prompt); x_sample = f(x_sample)
    shared = {
        "pre_mix_norm": f(pre_mix_norm).reshape(1, D), "w_in": f(w_in)[0], "rnn_conv_w": f(rnn_conv_w)[0],
        "rnn_conv_b": f(rnn_conv_b).reshape(1, 1024), "lru_wa": f(lru_wa)[0], "lru_ba": f(lru_ba).reshape(1, 1024),
        "lru_wx": f(lru_wx)[0], "lru_bx": f(lru_bx).reshape(1, 1024), "lru_lambda": f(lru_lambda).reshape(1, 1024),
        "hg_lb": f(hg_lb), "hg_norm": f(hg_norm).reshape(1, 128), "mem_norm": f(mem_norm).reshape(1, D),
        "w_mem_kv": f(w_mem_kv)[0], "w_branch": f(w_branch)[0], "b_gate": f(b_gate)[0], "w_out": f(w_out)[0],
        "post_mix_norm": f(post_mix_norm).reshape(1, D), "pre_ffn_norm": f(pre_ffn_norm).reshape(1, D),
        "w_ffn_up": f(w_ffn_up)[0], "ffn_conv_w": f(ffn_conv_w)[0], "ffn_conv_b": f(ffn_conv_b).reshape(1, FFN),
        "w_ffn_down": f(w_ffn_down)[0], "post_ffn_norm": f(post_ffn_norm).reshape(1, D),
    }
    ckk = f(cache_mem_k)[0].reshape(16, 256, 1024)
    cvv = f(cache_mem_v)[0].reshape(16, 256, 1024)
    in_maps = []
    for c in range(NCORES):
        bi, qi = c // 4, c % 4
        t0 = qi * 2048
        m = dict(shared)
        m["xp"] = x_prompt[bi, t0:t0 + 2048]
        m["xs"] = x_sample[2 * c:2 * c + 2].reshape(64, D)
        xh = np.zeros((3, D), np.float32)
        if qi > 0:
            xh[:] = x_prompt[bi, t0 - 3:t0]
        m["xhalo"] = xh
        m["mem"] = f(mem_prompt)[bi]
        m["ck"] = ckk[2 * c:2 * c + 2]
        m["cv"] = cvv[2 * c:2 * c + 2]
        m["st_h"] = f(state_rnn_h)[0, 2 * c:2 * c + 2]
        m["st_conv"] = f(state_rnn_conv)[0, 2 * c:2 * c + 2]
        m["st_hg"] = f(state_hg)[0, 2 * c:2 * c + 2]
        m["st_fconv"] = f(state_ffn_conv)[0, 2 * c:2 * c + 2]
        oh = np.zeros((P, NCORES), np.float32); oh[:, c] = 1.0
        pm = np.zeros((P, NCORES), np.float32); pm[:, bi * 4:c] = 1.0
        p1 = np.zeros((P, NCORES), np.float32)
        if qi > 0:
            p1[:, c - 1] = 1.0
        m["onehot"] = oh; m["predm"] = pm; m["prev1"] = p1
        in_maps.append({k: np.ascontiguousarray(v) for k, v in m.items()})
    return in_maps


def assemble(R):
    y_prompt = np.stack([np.concatenate([R[b * 4 + q]["y_p"] for q in range(4)], 0) for b in range(2)], 0)
    y_sample = np.concatenate([R[c]["y_s"].reshape(2, 32, D) for c in range(NCORES)], 0)
    mem_k = np.stack([R[b * 4]["mk_o"].reshape(256, 4, 256) for b in range(2)], 0)[None]
    mem_v = np.stack([R[b * 4]["mv_o"].reshape(256, 4, 256) for b in range(2)], 0)[None]
    rnn_h_p = np.stack([R[b * 4 + 3]["h_p_o"].reshape(1024) for b in range(2)], 0)[None]
    rnn_conv_p = np.stack([R[b * 4 + 3]["conv_p_o"] for b in range(2)], 0)[None]
    hg_p = np.stack([R[b * 4 + 3]["hg_p_o"] for b in range(2)], 0)[None]
    fconv_p = np.stack([R[b * 4 + 3]["fconv_p_o"] for b in range(2)], 0)[None]
    rnn_h_s = np.concatenate([R[c]["h_s_o"] for c in range(NCORES)], 0)[None]
    rnn_conv_s = np.concatenate([R[c]["conv_s_o"] for c in range(NCORES)], 0)[None]
    hg_s = np.concatenate([R[c]["hg_s_o"] for c in range(NCORES)], 0)[None]
    fconv_s = np.concatenate([R[c]["fconv_s_o"] for c in range(NCORES)], 0)[None]
    outs = (y_prompt, y_sample, mem_k, mem_v, rnn_h_p, rnn_conv_p, hg_p, fconv_p, rnn_h_s, rnn_conv_s, hg_s, fconv_s)
    return tuple(np.ascontiguousarray(o, dtype=np.float32) for o in outs)
```

```python
import numpy as np
import concourse.bass as bass
import concourse.mybir as mybir
from concourse.bass_utils import run_bass_kernel_spmd
from contextlib import ExitStack

F32 = mybir.dt.float32
BF16 = mybir.dt.bfloat16
AF = mybir.ActivationFunctionType
ALU = mybir.AluOpType
AX = mybir.AxisListType

ENGS = ("pe", "act", "dve", "pool", "sp")
N_DMA_HW = 16
N_DMA_SW = 8
N_DMA_SEMS = N_DMA_HW + N_DMA_SW
SAME_ENGINE_SYNC = True
RNN_LANES = 1


class Buf:
    __slots__ = ("name", "last_w", "readers")

    def __init__(self, name=""):
        self.name = name
        self.last_w = None
        self.readers = []


class Op:
    __slots__ = ("eng", "fn", "deps", "needs_inc", "kind", "dma_sem", "dma_val", "idx", "incval")

    def __init__(self, eng, fn, kind):
        self.eng = eng
        self.fn = fn
        self.deps = []
        self.needs_inc = False
        self.kind = kind
        self.dma_sem = None
        self.dma_val = None
        self.idx = None
        self.incval = None


class Sched:
    def __init__(self):
        self.ops = {e: [] for e in ENGS}
        self.dma_count = 0
        self.dma_count_sw = 0
        self.dma_last = [None] * N_DMA_SEMS
        self.dma_vals = [0] * N_DMA_SEMS
        self.cc_val = 0
        self.cc_last = None
        self.fence = None
        self.fenced = set()

    def barrier(self):
        f = [self.ops[e][-1] for e in ENGS if self.ops[e]]
        f += [o for o in self.dma_last if o is not None]
        if self.cc_last is not None:
            f.append(self.cc_last)
        self.fence = f
        self.fenced = {"pool"}

    def _add(self, eng, fn, reads, writes, kind):
        op = Op(eng, fn, kind)
        op.idx = len(self.ops[eng])
        deps = []
        if self.fence is not None and eng not in self.fenced:
            deps.extend(self.fence)
            self.fenced.add(eng)
        for b in reads:
            if b.last_w is not None:
                deps.append(b.last_w)
        for b in writes:
            if b.last_w is not None:
                deps.append(b.last_w)
            deps.extend(b.readers)
        if kind == "dma":
            if eng == "pool":
                k = N_DMA_HW + self.dma_count_sw % N_DMA_SW
                self.dma_count_sw += 1
            else:
                k = self.dma_count % N_DMA_HW
                self.dma_count += 1
            prev = self.dma_last[k]
            if prev is not None:
                deps.append(prev)
            self.dma_vals[k] += 16
            op.dma_sem = k
            op.dma_val = self.dma_vals[k]
            self.dma_last[k] = op
        elif kind == "cc":
            if self.cc_last is not None:
                deps.append(self.cc_last)
            self.cc_val += 1
            op.dma_val = self.cc_val
            self.cc_last = op
        seen = set()
        best = {}
        for d in deps:
            if id(d) in seen or d is op:
                continue
            seen.add(id(d))
            if d.kind == "op":
                if d.eng == eng and (eng == "pe" or eng == "sp" or not SAME_ENGINE_SYNC):
                    continue
                if d.eng not in best or best[d.eng].idx < d.idx:
                    best[d.eng] = d
            else:
                op.deps.append(d)
        for d in best.values():
            d.needs_inc = True
            op.deps.append(d)
        self.ops[eng].append(op)
        for b in writes:
            b.last_w = op
            b.readers = []
        for b in reads:
            if b.last_w is not op:
                b.readers.append(op)
        return op

    def op(self, eng, fn, reads=(), writes=()):
        return self._add(eng, fn, reads, writes, "op")

    def dma(self, eng, fn, reads=(), writes=()):
        return self._add(eng, fn, reads, writes, "dma")

    def cc(self, fn, reads=(), writes=()):
        return self._add("pool", fn, reads, writes, "cc")

    def lower(self, nc, final_wait_eng="sp"):
        for e in ENGS:
            c = 0
            for op in self.ops[e]:
                if op.kind == "op" and op.needs_inc:
                    c += 1
                    op.incval = c
        with ExitStack() as es:
            esem = {e: es.enter_context(nc.semaphore("s_" + e)) for e in ENGS}
            dsem = [es.enter_context(nc.semaphore("d_%d" % i)) for i in range(N_DMA_SEMS)]
            csem = es.enter_context(nc.semaphore("ccs"))
            block = es.enter_context(nc.Block())
            sched = self

            def run(ename, engobj):
                waited_e = {a: 0 for a in ENGS}
                waited_d = [0] * N_DMA_SEMS
                waited_c = 0
                for op in sched.ops[ename]:
                    for d in op.deps:
                        if d.kind == "dma":
                            if waited_d[d.dma_sem] < d.dma_val:
                                engobj.wait_ge(dsem[d.dma_sem], d.dma_val)
                                waited_d[d.dma_sem] = d.dma_val
                        elif d.kind == "cc":
                            if waited_c < d.dma_val:
                                engobj.wait_ge(csem, d.dma_val)
                                waited_c = d.dma_val
                        else:
                            if waited_e[d.eng] < d.incval:
                                engobj.wait_ge(esem[d.eng], d.incval)
                                waited_e[d.eng] = d.incval
                    inst = op.fn(engobj)
                    if op.kind == "dma":
                        inst.then_inc(dsem[op.dma_sem], 16)
                    elif op.kind == "cc":
                        inst.then_inc(csem)
                    elif op.needs_inc:
                        inst.then_inc(esem[ename], 1)
                if ename == final_wait_eng:
                    for k in range(N_DMA_SEMS):
                        if sched.dma_vals[k] > 0:
                            engobj.wait_ge(dsem[k], sched.dma_vals[k])

            @block.tensor
            def _(e):
                run("pe", e)

            @block.scalar
            def _(e):
                run("act", e)

            @block.vector
            def _(e):
                run("dve", e)

            @block.gpsimd
            def _(e):
                run("pool", e)

            @block.sync
            def _(e):
                run("sp", e)


P = 128
D = 2048
KD = 16
NBLK = 4
PT = 512
ST = 16
TB = PT + ST
HALF = TB // 2
NS = ((0, HALF), (HALF, TB))
FFN = 5632
NF = 44
IN_COLS = 12288
EPS = 1e-6
NCORES = 8
TILE_ROWS = (128, 128, 128, 128, ST)
TILE_COL0 = (0, 128, 256, 384, 512)
SEGS = ((0, PT, 64), (PT, ST, ST))
SLOT = 544
NSLOT = 18
SUMW = 8 + 8 + 8 + 1024


class TT:
    def __init__(self, t, name=""):
        self.t = t
        self.b = Buf(name)

    def __getitem__(self, k):
        return self.t[k]


def build_program(debug=False):
    nc = bass.Bass("TRN2", target_bir_lowering=False)
    S = Sched()

    def din(name, shape):
        return nc.dram_tensor(name, list(shape), F32, kind="ExternalInput").ap()

    def dout(name, shape):
        return nc.dram_tensor(name, list(shape), F32, kind="ExternalOutput").ap()

    xp = din("xp", [NBLK * PT, D])
    xs = din("xs", [2 * 32, D])
    xhalo = din("xhalo", [3, D])
    mem = din("mem", [256, D])
    ck = din("ck", [2, 256, 1024])
    cv = din("cv", [2, 256, 1024])
    st_h = din("st_h", [2, 1024])
    st_conv = din("st_conv", [2, 3, 1024])
    st_hg = din("st_hg", [2, 8, 128, 128])
    st_fconv = din("st_fconv", [2, 2, FFN])
    onehot = din("onehot", [P, NCORES])
    predm = din("predm", [P, NCORES])
    prev1 = din("prev1", [P, NCORES])
    pre_mix_norm = din("pre_mix_norm", [1, D])
    w_in = din("w_in", [D, IN_COLS])
    rnn_conv_w = din("rnn_conv_w", [4, 1024])
    rnn_conv_b = din("rnn_conv_b", [1, 1024])
    lru_wa = din("lru_wa", [16, 64, 64])
    lru_ba = din("lru_ba", [1, 1024])
    lru_wx = din("lru_wx", [16, 64, 64])
    lru_bx = din("lru_bx", [1, 1024])
    lru_lambda = din("lru_lambda", [1, 1024])
    hg_lb = din("hg_lb", [2, 1024])
    hg_norm = din("hg_norm", [1, 128])
    mem_norm = din("mem_norm", [1, D])
    w_mem_kv = din("w_mem_kv", [D, 2048])
    w_branch = din("w_branch", [3, 1024, D])
    b_gate = din("b_gate", [3, D])
    w_out = din("w_out", [D, D])
    post_mix_norm = din("post_mix_norm", [1, D])
    pre_ffn_norm = din("pre_ffn_norm", [1, D])
    w_ffn_up = din("w_ffn_up", [D, 2 * FFN])
    ffn_conv_w = din("ffn_conv_w", [3, FFN])
    ffn_conv_b = din("ffn_conv_b", [1, FFN])
    w_ffn_down = din("w_ffn_down", [FFN, D])
    post_ffn_norm = din("post_ffn_norm", [1, D])

    y_p = dout("y_p", [NBLK * PT, D])
    y_s = dout("y_s", [64, D])
    mk_o = dout("mk_o", [256, 1024])
    mv_o = dout("mv_o", [256, 1024])
    h_p_o = dout("h_p_o", [1, 1024])
    conv_p_o = dout("conv_p_o", [3, 1024])
    hg_p_o = dout("hg_p_o", [8, 128, 128])
    fconv_p_o = dout("fconv_p_o", [2, FFN])
    h_s_o = dout("h_s_o", [2, 1024])
    conv_s_o = dout("conv_s_o", [2, 3, 1024])
    hg_s_o = dout("hg_s_o", [2, 8, 128, 128])
    fconv_s_o = dout("fconv_s_o", [2, 2, FFN])

    x1d = nc.dram_tensor("x1d", [NBLK * TB, D], F32).ap()
    ar1_in = nc.dram_tensor("ar1_in", [P, NCORES * SUMW], F32)
    ar1_out = nc.dram_tensor("ar1_out", [P, NCORES * SUMW], F32)
    ar2_in = nc.dram_tensor("ar2_in", [P, NCORES * 88], F32)
    ar2_out = nc.dram_tensor("ar2_out", [P, NCORES * 88], F32)

    es = ExitStack()
    with es:
        def sb(name, shape, dt=F32):
            return TT(es.enter_context(nc.sbuf_tensor(name, list(shape), dt)), name)

        xT = sb("xT", [P, KD, TB], BF16)
        A2 = sb("A2", [P, NF * TB], BF16)
        A3 = sb("A3", [P, 10240], F32)
        slabs = [sb("slab%d" % i, [P, 5632], BF16) for i in range(3)]
        kT_p = sb("kT_p", [P, 8, 256], BF16)
        v_p = sb("v_p", [P, 2, 1024], BF16)
        kT_s = sb("kT_s", [P, 8, 256], BF16)
        v_s = sb("v_s", [P, 2, 1024], BF16)
        xts = [sb("xt0", [P, D], F32), sb("xt1", [P, D], F32)]
        xsbs = [sb("xsb0", [P, D], BF16), sb("xsb1", [P, D], BF16)]
        nrms = [sb("nrm0", [P, 2], F32), sb("nrm1", [P, 2], F32)]
        xt, xsb = xts[0], xsbs[0]
        gtok = sb("gtok", [P, D], F32)
        gpm = gtok
        gpf = gtok
        xtc = [0]
        ident = sb("ident", [P, P], BF16)
        identf = sb("identf", [P, P], F32)
        ones_bf = sb("ones_bf", [P, P], BF16)
        triT = sb("triT", [64, 64], F32)
        rstm = sb("rstm", [P, TB], F32)
        g_pre = sb("g_pre", [P, KD], F32)
        g_ffn = sb("g_ffn", [P, KD], F32)
        g_mem = sb("g_mem", [P, KD], F32)
        cw = sb("cw", [P, 8, 4], F32)
        cb = sb("cb", [P, 8], F32)
        ba = sb("ba", [P, 8], F32)
        bx = sb("bx", [P, 8], F32)
        lamc = sb("lamc", [P, 8], F32)
        lamc2 = sb("lamc2", [P, 8], F32)
        lbt = sb("lbt", [P, 8], F32)
        omlb = sb("omlb", [P, 8], F32)
        lb2 = sb("lb2", [P, 2, 8], F32)
        hgn = sb("hgn", [P, 1], F32)
        bgt = sb("bgt", [P, 3, KD], F32)
        fcw = sb("fcw", [P, NF, 3], F32)
        fcb = sb("fcb", [P, NF], F32)
        wa_bd = sb("wa_bd", [P, 8, P], BF16)
        wx_bd = sb("wx_bd", [P, 8, P], BF16)
        rsm = [sb("rsm0", [P, 1], F32), sb("rsm1", [P, 1], F32)]
        ohm = sb("ohm", [P, NCORES], F32)
        pdm = sb("pdm", [P, NCORES], F32)
        pv1 = sb("pv1", [P, NCORES], F32)
        small = sb("small", [P, 64], F32)
        hst = [sb("h_p", [P, 8], F32), sb("h_s", [P, 8], F32)]
        Sst = [sb("S_p", [P, 8, P], F32), sb("S_s", [P, 8, P], F32)]
        czr = [sb("czr_p", [P, 8, 3], F32), sb("czr_s", [P, 8, 3], F32)]
        cuf = [sb("cu_p", [P, NF, 2], F32), sb("cu_s", [P, NF, 2], F32)]
        HB = [[Buf("hB%d_%d" % (i, c)) for c in range(8)] for i in range(2)]
        CB = [[Buf("cB%d_%d" % (i, c)) for c in range(8)] for i in range(2)]
        SBH = [[Buf("sB%d_%d" % (i, c)) for c in range(8)] for i in range(2)]
        RSB = [Buf("rsB%d" % c) for c in range(8)]
        rsum = sb("rsum", [P, 8], F32)
        gsum = sb("gsum", [P, 8], F32)
        smb = sb("smb", [P, P], BF16)
        hsm = sb("hsm", [P, 40], F32)
        ycT = sb("ycT", [P, TB], F32)
        tokw = sb("tokw", [64, 2, P], BF16)
        amt = sb("amt", [64, 64], BF16)
        dbgf = sb("dbgf", [P, 512], F32) if debug else None
        ps = TT(es.enter_context(nc.psum_tensor("ps", [P, 6, 512], F32)), "ps")
        psb = TT(es.enter_context(nc.psum_tensor("psb", [P, 2, 1024], BF16)), "psb")
        PB = [Buf("pb%d" % i) for i in range(6)]
        PBB = [Buf("pbb%d" % i) for i in range(2)]

        def slot(i, n=SLOT, dt=F32):
            v = A3.t[:, i * SLOT:(i + 1) * SLOT]
            if dt is BF16:
                v = v.bitcast(BF16)
                return v[:, 0:n]
            return v[:, 0:n]

        TB_ = [Buf("tmp%d" % i) for i in range(NSLOT)]
        ytok = A3.t[:, 0:5 * D].rearrange("p (t d) -> p t d", t=5)
        YB = Buf("ytok")
        YTB = [Buf("ytokt%d" % i) for i in range(5)]

        yA = A2.t[:, 0:8 * TB].rearrange("p (c t) -> p c t", c=8)
        yB = A2.t[:, 8 * TB:16 * TB].rearrange("p (c t) -> p c t", c=8)
        yC = A2.t[:, 16 * TB:24 * TB].rearrange("p (c t) -> p c t", c=8)
        mT = A2.t[:, 24 * TB:40 * TB].rearrange("p (c t) -> p c t", c=16)
        actT = A2.t[:, 0:NF * TB].rearrange("p (c t) -> p c t", c=NF)
        YAB = [[Buf("yA%d" % i) for i in range(8)], [Buf("yB%d" % i) for i in range(8)], [Buf("yC%d" % i) for i in range(8)]]
        MTB = [Buf("mT%d" % i) for i in range(16)]
        ACTB = [Buf("act%d" % i) for i in range(NF)]
        DR = Buf("dram_misc")
        X1B = [Buf("x1d%d" % i) for i in range(NBLK)]

        dbg_list = []

        def dbg(name, view, R):
            if not debug or (isinstance(debug, (list, tuple, set)) and name not in debug):
                return
            shp = list(view.shape)
            d = nc.dram_tensor("dbg_" + name, shp, F32, kind="ExternalOutput").ap()
            if view.dtype != F32:
                npart = shp[0]
                n = 1
                for q_ in shp[1:]:
                    n *= q_
                sc = dbgf.t[0:npart, 0:n]
                if len(shp) == 3:
                    sc = sc.rearrange("p (a b) -> p a b", a=shp[1])
                cp("dve", sc, view, R, [dbgf])
                dma("sp", lambda e: e.dma_start(out=d, in_=sc), [dbgf], [DR])
            else:
                dma("sp", lambda e: e.dma_start(out=d, in_=view), R, [DR])
            dbg_list.append(name)

        def bl(items):
            out = []
            for it in items:
                if it is None:
                    continue
                if isinstance(it, (list, tuple)):
                    out.extend(bl(it))
                else:
                    out.append(it.b if isinstance(it, TT) else it)
            return out

        def op(eng, fn, R, W):
            return S.op(eng, fn, bl(R), bl(W))

        def dma(eng, fn, R, W, nonc=False):
            if nonc:
                def f2(e, fn=fn):
                    with nc.allow_non_contiguous_dma(reason="small strided parameter/state layout"):
                        return fn(e)
                return S.dma(eng, f2, bl(R), bl(W))
            return S.dma(eng, fn, bl(R), bl(W))

        def mm(out, lhsT, rhs, start, stop, R, W):
            return op("pe", lambda e: e.matmul(out, lhsT=lhsT, rhs=rhs, start=start, stop=stop), R, W)

        def tr(out, in_, idt, R, W):
            return op("pe", lambda e: e.transpose(out=out, in_=in_, identity=idt), R, W)

        def act(out, in_, func, R, W, **kw):
            return op("act", lambda e: e.activation(out=out, in_=in_, func=func, **kw), R, W)

        def ts(eng, out, in0, s1, s2, op0, op1, R, W):
            if s2 is None:
                return op(eng, lambda e: e.tensor_scalar(out=out, in0=in0, scalar1=s1, scalar2=None, op0=op0), R, W)
            return op(eng, lambda e: e.tensor_scalar(out=out, in0=in0, scalar1=s1, scalar2=s2, op0=op0, op1=op1), R, W)

        def tt(eng, out, in0, in1, o, R, W):
            return op(eng, lambda e: e.tensor_tensor(out=out, in0=in0, in1=in1, op=o), R, W)

        def stt(eng, out, in0, sc, in1, op0, op1, R, W):
            return op(eng, lambda e: e.scalar_tensor_tensor(out=out, in0=in0, scalar=sc, in1=in1, op0=op0, op1=op1), R, W)

        def cp(eng, out, in_, R, W):
            if eng == "act":
                return op(eng, lambda e: e.activation(out=out, in_=in_, func=AF.Copy), R, W)
            return op(eng, lambda e: e.tensor_copy(out=out, in_=in_), R, W)

        def pair(i):
            return ps.t[:, 2 * i:2 * i + 2, 0:HALF], [PB[2 * i], PB[2 * i + 1]]

        def fm2(ap):
            return ap.rearrange("p (a b) -> p a b", a=2)

        CST = Buf("consts")
        op("pool", lambda e: e.memset(identf.t[:], 0.0), [], [identf])
        op("pool", lambda e: e.affine_select(out=identf.t[:], in_=identf.t[:], pattern=[[-1, P]], compare_op=ALU.not_equal,
                                             fill=1.0, base=0, channel_multiplier=1), [identf], [identf])
        cp("pool", ident.t[:], identf.t[:], [identf], [ident])
        op("pool", lambda e: e.memset(ones_bf.t[:], 1.0), [], [ones_bf])
        op("pool", lambda e: e.memset(triT.t[:], 1.0), [], [triT])
        op("pool", lambda e: e.affine_select(out=triT.t[:], in_=triT.t[:], pattern=[[1, 64]], compare_op=ALU.is_ge,
                                             fill=0.0, base=0, channel_multiplier=-1), [triT], [triT])
        op("pool", lambda e: e.memset(rstm.t[:], 1.0), [], [rstm])
        for (c0, ln, ch) in SEGS:
            for k in range(ln // ch):
                op("pool", lambda e, c=c0 + k * ch: e.memset(rstm.t[:, c:c + 1], 0.0), [rstm], [rstm])

        def ld_fm(dst, src, nchunk):
            dma("sp", lambda e: e.dma_start(out=dst, in_=src.rearrange("o (c p) -> p (o c)", p=P)), [], [CST], nonc=True)


        def ld_fm3(dst, src, buf, wbufs):
            J = src.shape[0]
            for j in range(J):
                dma("sp", lambda e, j=j: e.dma_start(out=dst[:, :, j], in_=src[j:j + 1, :].rearrange("o (c p) -> p (o c)", p=P)), buf, wbufs, nonc=True)

        def st_fm3(dst, src, rbufs):
            J = dst.shape[0]
            for j in range(J):
                dma("sp", lambda e, j=j: e.dma_start(out=dst[j:j + 1, :].rearrange("o (c p) -> p (o c)", p=P), in_=src[:, :, j]), rbufs, [DR], nonc=True)

        ld_fm(g_pre.t[:], pre_mix_norm, KD)
        ld_fm(g_ffn.t[:], pre_ffn_norm, KD)
        ld_fm(g_mem.t[:], mem_norm, KD)
        ld_fm(cb.t[:], rnn_conv_b, 8)
        ld_fm(ba.t[:], lru_ba, 8)
        ld_fm(bx.t[:], lru_bx, 8)
        ld_fm(lamc.t[:], lru_lambda, 8)
        ld_fm(fcb.t[:], ffn_conv_b, NF)
        ld_fm3(cw.t, rnn_conv_w, [], [CST])
        ld_fm3(fcw.t, ffn_conv_w, [], [CST])
        for l in range(2):
            dma("sp", lambda e, l=l: e.dma_start(out=lb2.t[:, l, :], in_=hg_lb[l:l + 1, :].rearrange("o (c p) -> p (o c)", p=P)), [], [CST], nonc=True)
        dma("sp", lambda e: e.dma_start(out=hgn.t[:], in_=hg_norm.rearrange("o p -> p o")), [], [CST], nonc=True)
        for n_ in range(3):
            dma("sp", lambda e, n_=n_: e.dma_start(out=bgt.t[:, n_, :], in_=b_gate[n_:n_ + 1, :].rearrange("o (c p) -> p (o c)", p=P)), [], [CST], nonc=True)
        dma("sp", lambda e: e.dma_start(out=gtok.t[:], in_=post_mix_norm.partition_broadcast(P)), [], [gtok])
        dma("sp", lambda e: e.dma_start(out=ohm.t[:], in_=onehot), [], [CST])
        dma("sp", lambda e: e.dma_start(out=pdm.t[:], in_=predm), [], [CST])
        dma("sp", lambda e: e.dma_start(out=pv1.t[:], in_=prev1), [], [CST])
        wstage = TT(A3.t[:, 0:8 * P].rearrange("p (c j) -> p c j", c=8), "wstage")
        for (wsrc, wdst) in ((lru_wa, wa_bd), (lru_wx, wx_bd)):
            op("pool", lambda e: e.memset(wstage.t[:], 0.0), [CST], [wstage])
            wv = wsrc.rearrange("(c two) i j -> two i c j", two=2)
            dma("sp", lambda e, wv=wv: e.dma_start(out=wstage.t[0:64, :, 0:64], in_=wv[0]), [], [wstage], nonc=True)
            dma("sp", lambda e, wv=wv: e.dma_start(out=wstage.t[64:128, :, 64:128], in_=wv[1]), [], [wstage], nonc=True)
            cp("pool", wdst.t[:], wstage.t[:], [wstage], [wdst])
        S.barrier()
        act(small.t[:, 0:8], lamc.t[:], AF.Exp, [CST], [small], scale=-1.0)
        act(small.t[:, 0:8], small.t[:, 0:8], AF.Ln, [small], [small], bias=1.0)
        ts("dve", lamc.t[:], small.t[:, 0:8], -8.0, None, ALU.mult, None, [small], [lamc])
        ts("dve", lamc2.t[:], small.t[:, 0:8], -16.0, None, ALU.mult, None, [small], [lamc2])
        tt("dve", small.t[:, 8:16], lb2.t[:, 0, :], lb2.t[:, 1, :], ALU.subtract, [CST], [small])
        act(lbt.t[:], small.t[:, 8:16], AF.Sigmoid, [small], [lbt])
        ts("dve", omlb.t[:], lbt.t[:], -1.0, 1.0, ALU.mult, ALU.add, [lbt], [omlb])
        S.barrier()

        slab_ctr = [0]
        SLQ = [Buf('slq0'), Buf('slq1')]

        def load_slab(src_fn_list):
            sl = slabs[slab_ctr[0] % 3]
            slab_ctr[0] += 1
            for (src, kc, ncols, off, width) in src_fn_list:
                dst = sl.t[:, 0:kc * width].rearrange("p (k n) -> p k n", k=kc)[:, :, off:off + ncols]
                gate = SLQ[slab_ctr[0] % 2]
                dma("pool", lambda e, dst=dst, src=src: e.dma_start(out=dst, in_=src.rearrange("(k p) n -> p k n", p=P)), [], [sl, gate])
            return sl

        pair_ctr = [0]

        def next_pair():
            i = pair_ctr[0] % 3
            pair_ctr[0] += 1
            return pair(i)

        def linear_chunk(sl, kc, width, off, inT_fn, inR, col_splits=NS):
            pv, pbufs = next_pair()
            wv = sl.t[:, 0:kc * width].rearrange("p (k n) -> p k n", k=kc)
            for si, (a, b) in enumerate(col_splits):
                for k in range(kc):
                    mm(pv[:, si, 0:b - a], wv[:, k, off:off + P], inT_fn(k, a, b), k == 0, k == kc - 1,
                       [sl] + inR, [pbufs[si]])
            return pv, pbufs

        XTB = xT.b

        def prenorm(b, src_fn, gain, dstT, dstbuf, srcR):
            for t5 in range(5):
                rows = TILE_ROWS[t5]
                c0 = TILE_COL0[t5]
                xi = xtc[0] % 2
                xtc[0] += 1
                xt, xsb, nrm = xts[xi], xsbs[xi], nrms[xi]
                dma("sp", lambda e, t5=t5, rows=rows, xt=xt: e.dma_start(out=xt.t[0:rows, :], in_=src_fn(t5)), srcR, [xt])
                act(xsb.t[0:rows, :], xt.t[0:rows, :], AF.Square, [xt], [xsb, nrm], accum_out=nrm.t[0:rows, 0:1])
                act(nrm.t[0:rows, 1:2], nrm.t[0:rows, 0:1], AF.Sqrt, [nrm], [nrm], scale=1.0 / D, bias=EPS)
                op("dve", lambda e, rows=rows, nrm=nrm: e.reciprocal(out=nrm.t[0:rows, 1:2], in_=nrm.t[0:rows, 1:2]), [nrm], [nrm])
                ts("dve", xsb.t[0:rows, :], xt.t[0:rows, :], nrm.t[0:rows, 1:2], None, ALU.mult, None, [xt, nrm], [xsb])
                for g2 in range(2):
                    pvb = psb.t[:, g2, :].rearrange("p (j r) -> p j r", j=8)
                    for j in range(8):
                        kc = g2 * 8 + j
                        tr(pvb[:, j, 0:rows], xsb.t[0:rows, kc * P:(kc + 1) * P], ident.t[0:rows, 0:rows], [xsb, ident], [PBB[g2]])
                    gv = gain.t[:, g2 * 8:(g2 + 1) * 8].unsqueeze(2).broadcast_to([P, 8, rows])
                    tt("dve", dstT[:, g2 * 8:(g2 + 1) * 8, c0:c0 + rows], pvb[:, :, 0:rows], gv, ALU.mult, [PBB[g2], CST], [dstbuf])

        def x_src(b):
            def f(t5):
                if t5 < 4:
                    return xp[b * PT + t5 * P: b * PT + (t5 + 1) * P, :]
                r0 = (b // 2) * 32 + (b % 2) * ST
                return xs[r0:r0 + ST, :]
            return f

        def x1_src(b):
            def f(t5):
                r0 = b * TB + TILE_COL0[t5]
                return x1d[r0:r0 + TILE_ROWS[t5], :]
            return f

        def build_kv_from_rows(get_k_rows, get_v_rows, kT, vv):
            for mt in range(2):
                kr, kR = get_k_rows(mt)
                vr, vR = get_v_rows(mt)
                cp("dve", vv.t[:, mt, :], vr, vR, [vv])
                for g in range(2):
                    bank = 4 + g
                    pvv = ps.t[:, bank, :].rearrange("p (j r) -> p j r", j=4)
                    for j in range(4):
                        c = g * 4 + j
                        tr(pvv[:, j, :], kr[:, c * P:(c + 1) * P], identf.t[:], kR + [identf], [PB[bank]])
                    act(kT.t[:, g * 4:(g + 1) * 4, mt * P:(mt + 1) * P], pvv, AF.Copy, [PB[bank]], [kT])

        memT = xT.t[:, :, 0:256]
        for mt in range(2):
            dma("sp", lambda e, mt=mt: e.dma_start(out=xt.t[:], in_=mem[mt * P:(mt + 1) * P, :]), [], [xt])
            act(xsb.t[:], xt.t[:], AF.Square, [xt], [xsb, small], accum_out=small.t[:, 16:17])
            act(small.t[:, 17:18], small.t[:, 16:17], AF.Sqrt, [small], [small], scale=1.0 / D, bias=EPS)
            op("dve", lambda e: e.reciprocal(out=small.t[:, 17:18], in_=small.t[:, 17:18]), [small], [small])
            ts("dve", xsb.t[:], xt.t[:], small.t[:, 17:18], None, ALU.mult, None, [xt, small], [xsb])
            for g2 in range(2):
                pvb = psb.t[:, g2, :].rearrange("p (j r) -> p j r", j=8)
                for j in range(8):
                    kc = g2 * 8 + j
                    tr(pvb[:, j, :], xsb.t[:, kc * P:(kc + 1) * P], ident.t[:], [xsb, ident], [PBB[g2]])
                gv = g_mem.t[:, g2 * 8:(g2 + 1) * 8].unsqueeze(2).broadcast_to([P, 8, P])
                tt("dve", memT[:, g2 * 8:(g2 + 1) * 8, mt * P:(mt + 1) * P], pvb, gv, ALU.mult, [PBB[g2], CST], [xT])
        kvrow = A3.t[:, 0:2 * 2048].rearrange("p (m d) -> p m d", m=2)
        KVB = Buf("kvrow")
        for cbk in range(8):
            sl = load_slab([(w_mem_kv[:, cbk * 256:(cbk + 1) * 256], KD, 256, 0, 256)])
            wv = sl.t[:, 0:KD * 256].rearrange("p (k n) -> p k n", k=KD)
            for mt in range(2):
                bank = (cbk * 2 + mt) % 4
                for k in range(KD):
                    mm(ps.t[:, bank, 0:256], memT[:, k, mt * P:(mt + 1) * P], wv[:, k, :], k == 0, k == KD - 1, [sl, xT], [PB[bank]])
                act(kvrow[:, mt, cbk * 256:(cbk + 1) * 256], ps.t[:, bank, 0:256], AF.Copy, [PB[bank]], [KVB])
        for mt in range(2):
            dma("sp", lambda e, mt=mt: e.dma_start(out=mk_o[mt * P:(mt + 1) * P, :], in_=kvrow[:, mt, 0:1024]), [KVB], [DR])
            dma("sp", lambda e, mt=mt: e.dma_start(out=mv_o[mt * P:(mt + 1) * P, :], in_=kvrow[:, mt, 1024:2048]), [KVB], [DR])
        build_kv_from_rows(lambda mt: (kvrow[:, mt, 0:1024], [KVB]), lambda mt: (kvrow[:, mt, 1024:2048], [KVB]), kT_p, v_p)
        S.barrier()

        def load_sample_kv(seq):
            ckr = A3.t[:, 0:2 * 1024].rearrange("p (m d) -> p m d", m=2)
            cvr = A3.t[:, 2048:2048 + 2 * 1024].rearrange("p (m d) -> p m d", m=2)
            dma("sp", lambda e: e.dma_start(out=ckr, in_=ck[seq].rearrange("(m p) d -> p m d", p=P)), [], [KVB])
            dma("sp", lambda e: e.dma_start(out=cvr, in_=cv[seq].rearrange("(m p) d -> p m d", p=P)), [], [KVB])
            build_kv_from_rows(lambda mt: (ckr[:, mt, :], [KVB]), lambda mt: (cvr[:, mt, :], [KVB]), kT_s, v_s)
            S.barrier()

        def zero_states(which):
            op("pool", lambda e: e.memset(hst[which].t[:], 0.0), [], [HB[which]])
            op("pool", lambda e: e.memset(Sst[which].t[:], 0.0), [], [SBH[which]])

        def load_sample_state(seq):
            dma("sp", lambda e: e.dma_start(out=hst[1].t[:], in_=st_h[seq:seq + 1, :].rearrange("o (c p) -> p (o c)", p=P)), [], [HB[1]], nonc=True)
            ld_fm3(czr[1].t, st_conv[seq], [], [CB[1]])
            dma("sp", lambda e: e.dma_start(out=Sst[1].t[:], in_=st_hg[seq].rearrange("h k v -> k h v")), [], [SBH[1]])
            ld_fm3(cuf[1].t, st_fconv[seq], [], [cuf[1]])

        def store_sample_state_mix(seq):
            dma("sp", lambda e: e.dma_start(out=h_s_o[seq:seq + 1, :].rearrange("o (c p) -> p (o c)", p=P), in_=hst[1].t[:]), [HB[1]], [DR], nonc=True)
            st_fm3(conv_s_o[seq], czr[1].t, [CB[1]])
            dma("sp", lambda e: e.dma_start(out=hg_s_o[seq].rearrange("h k v -> k h v"), in_=Sst[1].t[:]), [SBH[1]], [DR])

        def run_lanes(factories, nl):
            pending = list(factories)
            lanes = [None] * nl
            while True:
                active = False
                for i in range(nl):
                    if lanes[i] is None and pending:
                        lanes[i] = pending.pop(0)(i)
                    if lanes[i] is not None:
                        active = True
                        try:
                            next(lanes[i])
                        except StopIteration:
                            lanes[i] = None
                if not active and not pending:
                    break

        def rnn_branch(b, summary, halo_first):
            segs = SEGS[:1] if summary else SEGS
            ZOFF = (0, 3 + PT)
            slc = {}

            def get_sl(pr):
                if pr not in slc:
                    slc[pr] = load_slab([(w_in[:, pr * 256:(pr + 1) * 256], KD, 256, 0, 256)])
                return slc[pr]

            def chunk(ch, lane):
                sl = get_sl(ch // 2)
                q = ch % 2
                sb_ = lane * 9
                T_ = [TB_[sb_ + i] for i in range(8)]
                zt = slot(sb_ + 0, 3 + PT + 3 + ST)
                pv, pbufs = linear_chunk(sl, KD, 256, q * P, lambda k, a, b_: xT.t[:, k, a:b_], [xT])
                yield
                act(zt[:, 3:3 + HALF], pv[:, 0, :], AF.Copy, [pbufs[0]], [T_[0]])
                act(zt[:, 3 + HALF:3 + PT], pv[:, 1, 0:PT - HALF], AF.Copy, [pbufs[1]], [T_[0]])
                if not summary:
                    act(zt[:, 3 + PT + 3:3 + PT + 3 + ST], pv[:, 1, PT - HALF:HALF], AF.Copy, [pbufs[1]], [T_[0]])
                yield
                if halo_first:
                    wv = sl.t[:, 0:KD * 256].rearrange("p (k n) -> p k n", k=KD)
                    bank = 4 + lane
                    for k in range(KD):
                        mm(ps.t[:, bank, 0:3], wv[:, k, q * P:(q + 1) * P], xhT[:, k, 0:3], k == 0, k == KD - 1, [sl, XH], [PB[bank]])
                    act(zt[:, 0:3], ps.t[:, bank, 0:3], AF.Copy, [PB[bank]], [T_[0]])
                else:
                    cp("act", zt[:, 0:3], czr[0].t[:, ch, :], [CB[0][ch]], [T_[0]])
                if not summary:
                    cp("act", zt[:, 3 + PT:3 + PT + 3], czr[1].t[:, ch, :], [CB[1][ch]], [T_[0]])
                yield
                xr = slot(sb_ + 1, TB)
                for si, (c0, ln, _) in enumerate(segs):
                    z0 = ZOFF[si]
                    ts("dve", xr[:, c0:c0 + ln], zt[:, z0:z0 + ln], cw.t[:, ch, 0:1], cb.t[:, ch:ch + 1], ALU.mult, ALU.add, [T_[0], CST], [T_[1]])
                    yield
                    for j in range(1, 4):
                        stt("dve", xr[:, c0:c0 + ln], zt[:, z0 + j:z0 + j + ln], cw.t[:, ch, j:j + 1], xr[:, c0:c0 + ln], ALU.mult, ALU.add, [T_[0], T_[1], CST], [T_[1]])
                        yield
                cp("act", czr[0].t[:, ch, :], zt[:, PT:PT + 3], [T_[0]], [CB[0][ch]])
                if not summary:
                    cp("act", czr[1].t[:, ch, :], zt[:, 3 + PT + ST:3 + PT + ST + 3], [T_[0]], [CB[1][ch]])
                ncols = PT if summary else TB
                xrb = slot(sb_ + 2, TB, BF16)
                act(xrb[:, 0:ncols], xr[:, 0:ncols], AF.Copy, [T_[1]], [T_[2]])
                yield
                splits = ((0, 256), (256, 512)) if summary else NS
                rr = slot(sb_ + 3, TB)
                ig = slot(sb_ + 4, TB)
                for (wbd, bias_t, dstv, dbuf) in ((wa_bd, ba, rr, T_[3]), (wx_bd, bx, ig, T_[4])):
                    pv2, pb2 = next_pair()
                    for si, (a, b_) in enumerate(splits):
                        mm(pv2[:, si, 0:b_ - a], wbd.t[:, ch, :], xrb[:, a:b_], True, True, [wbd, T_[2]], [pb2[si]])
                    yield
                    for si, (a, b_) in enumerate(splits):
                        act(dstv[:, a:b_], pv2[:, si, 0:b_ - a], AF.Sigmoid, [pb2[si], CST], [dbuf], bias=bias_t.t[:, ch:ch + 1])
                    yield
                aa = slot(sb_ + 5, TB)
                s2 = slot(sb_ + 6, TB)
                act(aa[:, 0:ncols], rr[:, 0:ncols], AF.Exp, [T_[3], lamc], [T_[5]], scale=lamc.t[:, ch:ch + 1])
                act(s2[:, 0:ncols], rr[:, 0:ncols], AF.Exp, [T_[3], lamc2], [T_[6]], scale=lamc2.t[:, ch:ch + 1])
                yield
                tt("dve", ig[:, 0:ncols], ig[:, 0:ncols], xr[:, 0:ncols], ALU.mult, [T_[4], T_[1]], [T_[4]])
                act(s2[:, 0:ncols], s2[:, 0:ncols], AF.Sqrt, [T_[6]], [T_[6]], scale=-1.0, bias=1.0)
                yield
                tt("dve", ig[:, 0:ncols], ig[:, 0:ncols], s2[:, 0:ncols], ALU.mult, [T_[4], T_[6]], [T_[4]])
                yield
                hh = slot(sb_ + 7, TB)
                for si, (c0, ln, _) in enumerate(segs):
                    op("dve", lambda e, c0=c0, ln=ln, si=si, ch=ch, hh=hh, aa=aa, ig=ig: e.tensor_tensor_scan(
                        out=hh[:, c0:c0 + ln], data0=aa[:, c0:c0 + ln], data1=ig[:, c0:c0 + ln],
                        initial=hst[si].t[:, ch:ch + 1], op0=ALU.mult, op1=ALU.add), [T_[5], T_[4], HB[si][ch]], [T_[7]])
                    cp("dve", hst[si].t[:, ch:ch + 1], hh[:, c0 + ln - 1:c0 + ln], [T_[7]], [HB[si][ch]])
                    yield
                if summary:
                    op("dve", lambda e, ch=ch, rr=rr, lane=lane: e.reduce_sum(out=rsm[lane].t[:, 0:1], in_=rr[:, 0:PT], axis=AX.X), [T_[3]], [rsm[lane]])
                    tt("dve", rsum.t[:, ch:ch + 1], rsum.t[:, ch:ch + 1], rsm[lane].t[:, 0:1], ALU.add, [rsm[lane], RSB[ch]], [RSB[ch]])
                else:
                    act(yA[:, ch, :], hh[:, 0:TB], AF.Copy, [T_[7]], [YAB[0][ch]])
                yield

            run_lanes([(lambda lane, ch=ch: chunk(ch, lane)) for ch in range(8)], RNN_LANES)

        def hg_branch(b, summary):
            segs = SEGS[:1] if summary else SEGS
            ncols = TB
            splits = NS
            for hp in range(4):
                groups = (("f", 2048), ("i", 3072)) if summary else (("q", 1024), ("f", 2048), ("i", 3072), ("o", 4096))
                stage = {}
                for (nm, cbase) in groups:
                    sl = load_slab([(w_in[:, cbase + hp * 256: cbase + (hp + 1) * 256], KD, 256, 0, 256)])
                    for q in range(2):
                        pv, pbufs = linear_chunk(sl, KD, 256, q * P, lambda k, a, b_: xT.t[:, k, a:b_], [xT], col_splits=splits)
                        if nm == "q":
                            si_ = 0 + q
                            dst = slot(si_, TB)
                            fn_ = AF.Copy
                        elif nm == "f":
                            si_ = 2 + q
                            dst = slot(si_, TB)
                            fn_ = AF.Sigmoid
                        elif nm == "o":
                            si_ = 4 + q
                            dst = slot(si_, TB)
                            fn_ = AF.Sigmoid
                        else:
                            si_ = 6 + q
                            dst = slot(si_, TB, BF16)
                            fn_ = AF.Copy
                        for si, (a, b_) in enumerate(splits):
                            act(dst[:, a:b_], pv[:, si, 0:b_ - a], fn_, [pbufs[si]], [TB_[si_]])
                        stage[(nm, q)] = (dst, TB_[si_])
                for q in range(2):
                    h = hp * 2 + q
                    ff, ffb = stage[("f", q)]
                    vT, vTb = stage[("i", q)]
                    ts("dve", ff[:, 0:ncols], ff[:, 0:ncols], omlb.t[:, h:h + 1], lbt.t[:, h:h + 1], ALU.mult, ALU.add, [ffb, omlb, lbt], [ffb])
                    kk = slot(8, TB)
                    ts("dve", kk[:, 0:ncols], ff[:, 0:ncols], -1.0, 1.0, ALU.mult, ALU.add, [ffb], [TB_[8]])
                    act(ff[:, 0:ncols], ff[:, 0:ncols], AF.Ln, [ffb], [ffb])
                    D0 = summary and b == 0 and h == 0
                    if D0:
                        dbg("g", ff[:, 0:PT], [ffb])
                        dbg("kk", kk[:, 0:PT], [TB_[8]])
                        dbg("vT", vT[:, 0:PT], [vTb])
                    bc = slot(9, TB)
                    op("dve", lambda e, bc=bc, ff=ff, ncols=ncols: e.tensor_tensor_scan(
                        out=bc[:, 0:ncols], data0=rstm.t[:, 0:ncols], data1=ff[:, 0:ncols],
                        initial=0.0, op0=ALU.mult, op1=ALU.add), [rstm, ffb], [TB_[9]])
                    chunks = []
                    for si, (c0, ln, chn) in enumerate(segs):
                        for kx in range(ln // chn):
                            chunks.append((si, c0 + kx * chn, chn, len(chunks) if si == 0 else 8))
                    NPC = PT // 64
                    bcp = bc[:, 0:PT].rearrange("p (c t) -> p c t", c=NPC)
                    cp("dve", hsm.t[:, 0:NPC], bcp[:, :, 31], [TB_[9]], [hsm])
                    cp("dve", hsm.t[:, 8:9], bc[:, PT + ST // 2 - 1:PT + ST // 2], [TB_[9]], [hsm])
                    cp("dve", hsm.t[:, 18:18 + NPC], bcp[:, :, 63], [TB_[9]], [hsm])
                    cp("dve", hsm.t[:, 26:27], bc[:, TB - 1:TB], [TB_[9]], [hsm])
                    act(small.t[:, 24:33], hsm.t[:, 0:9], AF.Exp, [hsm], [small])
                    act(small.t[:, 33:42], hsm.t[:, 18:27], AF.Exp, [hsm], [small])
                    tt("dve", hsm.t[:, 27:36], hsm.t[:, 18:27], hsm.t[:, 0:9], ALU.subtract, [hsm], [hsm])
                    act(small.t[:, 42:51], hsm.t[:, 27:36], AF.Exp, [hsm], [small])
                    if summary:
                        op("dve", lambda e: e.reduce_sum(out=small.t[:, 21:22], in_=hsm.t[:, 18:18 + 8], axis=AX.X), [hsm], [small])
                        tt("dve", gsum.t[:, h:h + 1], gsum.t[:, h:h + 1], small.t[:, 21:22], ALU.add, [small, gsum], [gsum])
                    tt("dve", bcp, bcp, hsm.t[:, 0:NPC].unsqueeze(2).broadcast_to([P, NPC, 64]), ALU.subtract, [TB_[9], hsm], [TB_[9]])
                    ts("dve", bc[:, PT:TB], bc[:, PT:TB], hsm.t[:, 8:9], None, ALU.subtract, None, [TB_[9], hsm], [TB_[9]])
                    e2 = slot(10, TB)
                    act(e2[:, 0:TB], bc[:, 0:TB], AF.Exp, [TB_[9]], [TB_[10]], scale=-1.0)
                    ke = slot(11, TB, BF16)
                    tt("dve", ke[:, 0:TB], kk[:, 0:TB], e2[:, 0:TB], ALU.mult, [TB_[8], TB_[10]], [TB_[11]])
                    kd = slot(12, TB, BF16)
                    tt("dve", kd[:, 0:PT].rearrange("p (c t) -> p c t", c=NPC), ke[:, 0:PT].rearrange("p (c t) -> p c t", c=NPC),
                       small.t[:, 42:42 + NPC].unsqueeze(2).broadcast_to([P, NPC, 64]), ALU.mult, [TB_[11], small], [TB_[12]])
                    ts("dve", kd[:, PT:TB], ke[:, PT:TB], small.t[:, 50:51], None, ALU.mult, None, [TB_[11], small], [TB_[12]])
                    if D0:
                        dbg("bc", bc[:, 0:PT], [TB_[9]])
                        dbg("hsm", hsm.t[:], [hsm])
                        dbg("small", small.t[:], [small])
                        dbg("e2", e2[:, 0:PT], [TB_[10]])
                        dbg("ke", ke[:, 0:PT], [TB_[11]])
                        dbg("kd", kd[:, 0:PT], [TB_[12]])
                    if not summary:
                        zq, zqb = stage[("q", q)]
                        og, ogb = stage[("o", q)]
                        e1 = slot(10, TB)
                        act(e1[:, 0:TB], bc[:, 0:TB], AF.Exp, [TB_[9]], [TB_[10]])
                        qe = slot(13, TB, BF16)
                        tt("dve", qe[:, 0:TB], zq[:, 0:TB], e1[:, 0:TB], ALU.mult, [zqb, TB_[10]], [TB_[13]])
                        oo = slot(14, TB)
                    for (si, cc0, chn, ci) in chunks:
                        Sx = Sst[si]
                        SxB = SBH[si][h]
                        pvb = psb.t[0:chn, 0, 0:2 * P].rearrange("p (j r) -> p j r", j=2)
                        tr(pvb[:, 0, :], kd[:, cc0:cc0 + chn], ident.t[:], [TB_[12], ident], [PBB[0]])
                        tr(pvb[:, 1, :], vT[:, cc0:cc0 + chn], ident.t[:], [vTb, ident], [PBB[0]])
                        cp("dve", tokw.t[0:chn, 0:2, :], pvb, [PBB[0]], [tokw])
                        if not summary:
                            mm(ps.t[0:chn, 4, 0:chn], ke[:, cc0:cc0 + chn], qe[:, cc0:cc0 + chn], True, True, [TB_[11], TB_[13]], [PB[4]])
                            tt("dve", amt.t[0:chn, 0:chn], ps.t[0:chn, 4, 0:chn], triT.t[0:chn, 0:chn], ALU.mult, [PB[4], triT], [amt])
                            act(smb.t[:], Sx.t[:, h, :], AF.Copy, [SxB, small], [smb], scale=small.t[:, 24 + ci:25 + ci])
                            mm(ps.t[:, 5, 0:chn], smb.t[:], qe[:, cc0:cc0 + chn], True, False, [smb, TB_[13]], [PB[5]])
                            mm(ps.t[:, 5, 0:chn], tokw.t[0:chn, 1, :], amt.t[0:chn, 0:chn], False, True, [tokw, amt], [PB[5]])
                            act(oo[:, cc0:cc0 + chn], ps.t[:, 5, 0:chn], AF.Copy, [PB[5]], [TB_[14]])
                        mm(ps.t[:, 3, 0:P], tokw.t[0:chn, 0, :], tokw.t[0:chn, 1, :], True, True, [tokw], [PB[3]])
                        stt("dve", Sx.t[:, h, :], Sx.t[:, h, :], small.t[:, 33 + ci:34 + ci], ps.t[:, 3, 0:P], ALU.mult, ALU.add, [SxB, small, PB[3]], [SxB])
                        if D0 and ci == 0:
                            dbg("tokw", tokw.t[:, 0:2, :], [tokw])
                            dbg("S0", Sx.t[:, 0, :], [SxB])
                    if not summary:
                        osq = slot(15, TB, BF16)
                        act(osq[:, 0:TB], oo[:, 0:TB], AF.Square, [TB_[14]], [TB_[15]])
                        pv3, pb3 = next_pair()
                        rs_ = slot(16, TB)
                        for si, (a, b_) in enumerate(NS):
                            mm(pv3[:, si, :], ones_bf.t[:], osq[:, a:b_], True, True, [ones_bf, TB_[15]], [pb3[si]])
                            act(rs_[:, a:b_], pv3[:, si, :], AF.Sqrt, [pb3[si]], [TB_[16]], scale=1.0 / 128.0, bias=EPS)
                        op("dve", lambda e, rs_=rs_: e.reciprocal(out=rs_[:, 0:TB], in_=rs_[:, 0:TB]), [TB_[16]], [TB_[16]])
                        stt("dve", oo[:, 0:TB], oo[:, 0:TB], hgn.t[:, 0:1], rs_[:, 0:TB], ALU.mult, ALU.mult, [TB_[14], TB_[16], CST], [TB_[14]])
                        tt("dve", yB[:, h, :], oo[:, 0:TB], og[:, 0:TB], ALU.mult, [TB_[14], ogb], [YAB[1][h]])

        def xattn_branch(b):
            for a4 in range(4):
                sl = load_slab([(w_in[:, 5120 + a4 * 256: 5120 + (a4 + 1) * 256], KD, 256, 0, 256)])
                qxs = [slot(0, TB, BF16), slot(1, TB, BF16)]
                for q in range(2):
                    pv, pbufs = linear_chunk(sl, KD, 256, q * P, lambda k, a, b_: xT.t[:, k, a:b_], [xT])
                    act(fm2(qxs[q][:, 0:TB]), pv, AF.Copy, pbufs, [TB_[q]], scale=1.0 / 16.0)
                pT = [slot(2, TB, BF16), slot(3, TB, BF16)]
                for t5 in range(5):
                    rows = TILE_ROWS[t5]
                    c0 = TILE_COL0[t5]
                    kTx, vx = (kT_p, v_p) if t5 < 4 else (kT_s, v_s)
                    bank = 4 + (t5 % 2)
                    for dc in range(2):
                        mm(ps.t[0:rows, bank, 0:256], qxs[dc][:, c0:c0 + rows], kTx.t[:, a4 * 2 + dc, :], dc == 0, dc == 1, [TB_[dc], kTx], [PB[bank]])
                    op("dve", lambda e, rows=rows, bank=bank: e.reduce_max(out=small.t[0:rows, 52:53], in_=ps.t[0:rows, bank, 0:256], axis=AX.X), [PB[bank]], [small])
                    ts("dve", small.t[0:rows, 53:54], small.t[0:rows, 52:53], -1.0, None, ALU.mult, None, [small], [small])
                    pf = slot(4, 256)
                    act(pf[0:rows, :], ps.t[0:rows, bank, 0:256], AF.Exp, [PB[bank], small], [TB_[4], small], bias=small.t[0:rows, 53:54], accum_out=small.t[0:rows, 54:55])
                    op("dve", lambda e, rows=rows: e.reciprocal(out=small.t[0:rows, 55:56], in_=small.t[0:rows, 54:55]), [small], [small])
                    pn = slot(5, 256, BF16)
                    ts("dve", pn[0:rows, :], pf[0:rows, :], small.t[0:rows, 55:56], None, ALU.mult, None, [TB_[4], small], [TB_[5]])
                    pvb = psb.t[:, 1, 0:2 * P].rearrange("p (j r) -> p j r", j=2)
                    for mc in range(2):
                        tr(pvb[:, mc, 0:rows], pn[0:rows, mc * P:(mc + 1) * P], ident.t[0:rows, 0:rows], [TB_[5], ident], [PBB[1]])
                    for mc in range(2):
                        cp("dve", pT[mc][:, c0:c0 + rows], pvb[:, mc, 0:rows], [PBB[1]], [TB_[2 + mc]])
                for dc in range(2):
                    chn = a4 * 2 + dc
                    for (c0, ln, kvv, bank) in ((0, PT, v_p, 2), (PT, ST, v_s, 3)):
                        for mc in range(2):
                            mm(ps.t[:, bank, 0:ln], kvv.t[:, mc, chn * P:(chn + 1) * P], pT[mc][:, c0:c0 + ln], mc == 0, mc == 1, [kvv, TB_[2 + mc]], [PB[bank]])
                        act(yC[:, chn, c0:c0 + ln], ps.t[:, bank, 0:ln], AF.Copy, [PB[bank]], [YAB[2][chn]])

        def merge_gates(b):
            maccs = [slot(0, TB), slot(3, TB)]
            MB = [TB_[0], TB_[3]]
            sg = slot(1, TB)
            tmp = slot(2, TB)
            yv = (yA, yB, yC)
            for jp in range(8):
                for nb in range(3):
                    gsl = load_slab([(w_in[:, 6144 + nb * 2048 + jp * 256: 6144 + nb * 2048 + (jp + 1) * 256], KD, 256, 0, 256)])
                    bsl = load_slab([(w_branch[nb][:, jp * 256:(jp + 1) * 256], 8, 256, 0, 256)])
                    ybr = yv[nb]
                    for q in range(2):
                        j = jp * 2 + q
                        macc = maccs[q]
                        pvg, pbg = linear_chunk(gsl, KD, 256, q * P, lambda k, a, b_: xT.t[:, k, a:b_], [xT])
                        act(fm2(sg[:, 0:TB]), pvg, AF.Sigmoid, pbg + [CST], [TB_[1]], bias=bgt.t[:, nb, j:j + 1])
                        pvp, pbp = linear_chunk(bsl, 8, 256, q * P, lambda k, a, b_, ybr=ybr: ybr[:, k, a:b_], YAB[nb])
                        if nb == 0:
                            tt("dve", fm2(macc[:, 0:TB]), pvp, fm2(sg[:, 0:TB]), ALU.mult, pbp + [TB_[1]], [MB[q]])
                        elif nb == 1:
                            tt("dve", fm2(tmp[:, 0:TB]), pvp, fm2(sg[:, 0:TB]), ALU.mult, pbp + [TB_[1]], [TB_[2]])
                            tt("dve", macc[:, 0:TB], macc[:, 0:TB], tmp[:, 0:TB], ALU.add, [MB[q], TB_[2]], [MB[q]])
                        else:
                            tt("dve", fm2(tmp[:, 0:TB]), pvp, fm2(sg[:, 0:TB]), ALU.mult, pbp + [TB_[1]], [TB_[2]])
                            tt("dve", mT[:, j, :], macc[:, 0:TB], tmp[:, 0:TB], ALU.add, [MB[q], TB_[2]], [MTB[j]])

        def project_tokmajor(wsrc, kc, inT, inBufs):
            for oc in range(16):
                if kc == KD:
                    if oc % 2 == 0:
                        sl = load_slab([(wsrc[:, oc * P:(oc + 2) * P], kc, 256, 0, 256)])
                        project_tokmajor.sl = sl
                    sl = project_tokmajor.sl
                    width, off = 256, (oc % 2) * P
                else:
                    sl = load_slab([(wsrc[:, oc * P:(oc + 1) * P], kc, P, 0, P)])
                    width, off = P, 0
                pv, pbufs = linear_chunk(sl, kc, width, off, lambda k, a, b_: inT[:, k, a:b_], inBufs)
                yc = ycT.t
                act(fm2(yc[:, 0:TB]), pv, AF.Copy, pbufs, [ycT])
                bank = oc % 2
                bnk = 4 + bank
                pvv = ps.t[:, bnk, :].rearrange("p (j r) -> p j r", j=4)
                for t5 in range(4):
                    tr(pvv[:, t5, :], yc[:, t5 * P:(t5 + 1) * P], identf.t[:], [ycT, identf], [PB[bnk]])
                act(ytok[:, 0:4, oc * P:(oc + 1) * P], pvv, AF.Copy, [PB[bnk]], [YB])
                tr(ps.t[0:ST, 3, 0:P], yc[:, PT:PT + ST], identf.t[:], [ycT, identf], [PB[3]])
                act(ytok[0:ST, 4, oc * P:(oc + 1) * P], ps.t[0:ST, 3, 0:P], AF.Copy, [PB[3]], [YB])

        def post_norm_residual(b, gain, res_src, resR, dst_fn, dstW):
            for t5 in range(5):
                rows = TILE_ROWS[t5]
                xi = xtc[0] % 2
                xtc[0] += 1
                xt, xsb, nrm = xts[xi], xsbs[xi], nrms[xi]
                YT = YTB[t5]
                dma("sp", lambda e, t5=t5, rows=rows, xt=xt: e.dma_start(out=xt.t[0:rows, :], in_=res_src(t5)), resR, [xt])
                act(xsb.t[0:rows, :], ytok[0:rows, t5, :], AF.Square, [YB, YT], [xsb, nrm], accum_out=nrm.t[0:rows, 0:1])
                act(nrm.t[0:rows, 1:2], nrm.t[0:rows, 0:1], AF.Sqrt, [nrm], [nrm], scale=1.0 / D, bias=EPS)
                op("dve", lambda e, rows=rows, nrm=nrm: e.reciprocal(out=nrm.t[0:rows, 1:2], in_=nrm.t[0:rows, 1:2]), [nrm], [nrm])
                stt("dve", ytok[0:rows, t5, :], ytok[0:rows, t5, :], nrm.t[0:rows, 1:2], gain.t[0:rows, :], ALU.mult, ALU.mult, [YB, YT, nrm, gain], [YT])
                tt("dve", xt.t[0:rows, :], xt.t[0:rows, :], ytok[0:rows, t5, :], ALU.add, [xt, YT], [xt])
                dma("sp", lambda e, t5=t5, rows=rows, xt=xt: e.dma_start(out=dst_fn(t5), in_=xt.t[0:rows, :]), [xt], dstW)

        xhT_t = sb("xhT", [P, KD, 4], BF16)
        xhT = xhT_t.t
        XH = xhT_t.b

        def build_xhalo():
            dma("sp", lambda e: e.dma_start(out=xt.t[0:3, :], in_=xhalo), [], [xt])
            act(xsb.t[0:3, :], xt.t[0:3, :], AF.Square, [xt], [xsb, small], accum_out=small.t[0:3, 16:17])
            act(small.t[0:3, 17:18], small.t[0:3, 16:17], AF.Sqrt, [small], [small], scale=1.0 / D, bias=EPS)
            op("dve", lambda e: e.reciprocal(out=small.t[0:3, 17:18], in_=small.t[0:3, 17:18]), [small], [small])
            ts("dve", xsb.t[0:3, :], xt.t[0:3, :], small.t[0:3, 17:18], None, ALU.mult, None, [xt, small], [xsb])
            for g2 in range(2):
                pvb = psb.t[:, g2, :].rearrange("p (j r) -> p j r", j=8)
                for j in range(8):
                    kc = g2 * 8 + j
                    tr(pvb[:, j, 0:3], xsb.t[0:3, kc * P:(kc + 1) * P], ident.t[0:3, 0:3], [xsb, ident], [PBB[g2]])
                gv = g_pre.t[:, g2 * 8:(g2 + 1) * 8].unsqueeze(2).broadcast_to([P, 8, 3])
                tt("dve", xhT[:, g2 * 8:(g2 + 1) * 8, 0:3], pvb[:, :, 0:3], gv, ALU.mult, [PBB[g2], CST], [XH])

        build_xhalo()
        zero_states(0)
        op("pool", lambda e: e.memset(rsum.t[:], 0.0), [], [RSB])
        op("pool", lambda e: e.memset(gsum.t[:], 0.0), [], [gsum])
        S.barrier()
        for b in range(NBLK):
            prenorm(b, x_src(b), g_pre, xT.t, XTB, [])
            rnn_branch(b, True, b == 0)
            hg_branch(b, True)
            S.barrier()
        sumv = A3.t[:, 0:NCORES * SUMW].rearrange("p (r w) -> p r w", r=NCORES)
        SUMB = Buf("sumv")
        mine = A3.t[:, NCORES * SUMW:NCORES * SUMW + SUMW]
        MINEB = Buf("mine")
        tt("dve", mine[:, 0:8], rsum.t[:], lamc.t[:], ALU.mult, [RSB, lamc], [MINEB])
        act(mine[:, 0:8], mine[:, 0:8], AF.Exp, [MINEB], [MINEB])
        cp("dve", mine[:, 8:16], hst[0].t[:], [HB[0]], [MINEB])
        act(mine[:, 16:24], gsum.t[:], AF.Exp, [gsum], [MINEB])
        cp("dve", mine[:, 24:24 + 1024], Sst[0].t[:].rearrange("p h v -> p (h v)"), [SBH[0]], [MINEB])
        for r in range(NCORES):
            ts("dve", sumv[:, r, :], mine, ohm.t[:, r:r + 1], None, ALU.mult, None, [MINEB, CST], [SUMB])
        AR1 = Buf("ar1")
        dma("pool", lambda e: e.dma_start(out=ar1_in.ap(), in_=A3.t[:, 0:NCORES * SUMW]), [SUMB], [AR1])
        S.cc(lambda e: e.collective_compute("AllReduce", ALU.add, replica_groups=[list(range(NCORES))],
                                            ins=[ar1_in.ap().opt()], outs=[ar1_out.ap().opt()]), [AR1], [AR1])
        dma("pool", lambda e: e.dma_start(out=A3.t[:, 0:NCORES * SUMW], in_=ar1_out.ap()), [AR1], [SUMB])
        zero_states(0)
        for r in range(NCORES):
            m_r = pdm.t[:, r:r + 1]
            ts("dve", small.t[:, 0:8], sumv[:, r, 0:8], -1.0, m_r, ALU.add, ALU.mult, [SUMB, CST], [small])
            ts("dve", small.t[:, 0:8], small.t[:, 0:8], 1.0, None, ALU.add, None, [small], [small])
            ts("dve", small.t[:, 8:16], sumv[:, r, 16:24], -1.0, m_r, ALU.add, ALU.mult, [SUMB, CST], [small])
            ts("dve", small.t[:, 8:16], small.t[:, 8:16], 1.0, None, ALU.add, None, [small], [small])
            tt("dve", hst[0].t[:], hst[0].t[:], small.t[:, 0:8], ALU.mult, [HB[0], small], [HB[0]])
            stt("dve", hst[0].t[:], sumv[:, r, 8:16], m_r, hst[0].t[:], ALU.mult, ALU.add, [SUMB, CST, HB[0]], [HB[0]])
            tt("dve", Sst[0].t[:], Sst[0].t[:], small.t[:, 8:16].unsqueeze(2).broadcast_to([P, 8, P]), ALU.mult, [SBH[0], small], [SBH[0]])
            stt("dve", Sst[0].t[:], sumv[:, r, 24:24 + 1024].rearrange("p (h v) -> p h v", h=8), m_r, Sst[0].t[:], ALU.mult, ALU.add,
                [SUMB, CST, SBH[0]], [SBH[0]])
        S.barrier()

        for b in range(NBLK):
            seq = b // 2
            if b % 2 == 0:
                load_sample_state(seq)
                load_sample_kv(seq)
            prenorm(b, x_src(b), g_pre, xT.t, XTB, [])
            rnn_branch(b, False, b == 0)
            S.barrier()
            hg_branch(b, False)
            S.barrier()
            xattn_branch(b)
            S.barrier()
            merge_gates(b)
            S.barrier()
            project_tokmajor(w_out, KD, mT, MTB)
            post_norm_residual(b, gpm, x_src(b), [], lambda t5, b=b: x1d[b * TB + TILE_COL0[t5]: b * TB + TILE_COL0[t5] + TILE_ROWS[t5], :], [X1B[b]])
            S.barrier()
            if b % 2 == 1:
                store_sample_state_mix(seq)
        dma("sp", lambda e: e.dma_start(out=h_p_o.rearrange("o (c p) -> p (o c)", p=P), in_=hst[0].t[:]), [HB[0]], [DR], nonc=True)
        st_fm3(conv_p_o, czr[0].t, [CB[0]])
        dma("sp", lambda e: e.dma_start(out=hg_p_o.rearrange("h k v -> k h v"), in_=Sst[0].t[:]), [SBH[0]], [DR])
        S.barrier()

        dma("sp", lambda e: e.dma_start(out=gtok.t[:], in_=post_ffn_norm.partition_broadcast(P)), [], [gtok])

        def ffn_u_chunk_slab(jp):
            return load_slab([(w_ffn_up[:, jp * 256:(jp + 1) * 256], KD, 256, 0, 256)])

        def last_tile_src(t5):
            r0 = 3 * TB + 384
            return x1d[r0:r0 + P, :]
        dma("sp", lambda e: e.dma_start(out=xt.t[:], in_=last_tile_src(0)), [X1B[3]], [xt])
        act(xsb.t[:], xt.t[:], AF.Square, [xt], [xsb, small], accum_out=small.t[:, 16:17])
        act(small.t[:, 17:18], small.t[:, 16:17], AF.Sqrt, [small], [small], scale=1.0 / D, bias=EPS)
        op("dve", lambda e: e.reciprocal(out=small.t[:, 17:18], in_=small.t[:, 17:18]), [small], [small])
        ts("dve", xsb.t[:], xt.t[:], small.t[:, 17:18], None, ALU.mult, None, [xt, small], [xsb])
        for g2 in range(2):
            pvb = psb.t[:, g2, :].rearrange("p (j r) -> p j r", j=8)
            for j in range(8):
                kc = g2 * 8 + j
                tr(pvb[:, j, :], xsb.t[:, kc * P:(kc + 1) * P], ident.t[:], [xsb, ident], [PBB[g2]])
            gv = g_ffn.t[:, g2 * 8:(g2 + 1) * 8].unsqueeze(2).broadcast_to([P, 8, P])
            tt("dve", xT.t[:, g2 * 8:(g2 + 1) * 8, 0:P], pvb, gv, ALU.mult, [PBB[g2], CST], [xT])
        uh = A3.t[:, NCORES * 88: NCORES * 88 + 88].rearrange("p (c j) -> p c j", c=NF)
        UHB = Buf("uh")
        for jp in range(22):
            sl = ffn_u_chunk_slab(jp)
            wv = sl.t[:, 0:KD * 256].rearrange("p (k n) -> p k n", k=KD)
            for q in range(2):
                j = jp * 2 + q
                bank = j % 2
                for k in range(KD):
                    mm(ps.t[:, bank, 0:2], wv[:, k, q * P:(q + 1) * P], xT.t[:, k, 126:128], k == 0, k == KD - 1, [sl, xT], [PB[bank]])
                act(uh[:, j, :], ps.t[:, bank, 0:2], AF.Copy, [PB[bank]], [UHB])
        uall = A3.t[:, 0:NCORES * 88].rearrange("p (r w) -> p r w", r=NCORES)
        UALL = Buf("uall")
        for r in range(NCORES):
            ts("dve", uall[:, r, :], uh.rearrange("p c j -> p (c j)"), ohm.t[:, r:r + 1], None, ALU.mult, None, [UHB, CST], [UALL])
        AR2 = Buf("ar2")
        dma("pool", lambda e: e.dma_start(out=ar2_in.ap(), in_=A3.t[:, 0:NCORES * 88]), [UALL], [AR2])
        S.cc(lambda e: e.collective_compute("AllReduce", ALU.add, replica_groups=[list(range(NCORES))],
                                            ins=[ar2_in.ap().opt()], outs=[ar2_out.ap().opt()]), [AR2], [AR2])
        dma("pool", lambda e: e.dma_start(out=A3.t[:, 0:NCORES * 88], in_=ar2_out.ap()), [AR2], [UALL])
        cufp = cuf[0].t[:].rearrange("p c j -> p (c j)")
        op("pool", lambda e: e.memset(cuf[0].t[:], 0.0), [], [cuf[0]])
        for r in range(NCORES):
            stt("dve", cufp, uall[:, r, :], pv1.t[:, r:r + 1], cufp, ALU.mult, ALU.add, [UALL, CST, cuf[0]], [cuf[0]])
        S.barrier()

        UOFF = (0, 2 + PT)
        for b in range(NBLK):
            seq = b // 2
            if b % 2 == 0:
                ld_fm3(cuf[1].t, st_fconv[seq], [], [cuf[1]])
            prenorm(b, x1_src(b), g_ffn, xT.t, XTB, [X1B[b]])
            for jp in range(22):
                slu = load_slab([(w_ffn_up[:, jp * 256:(jp + 1) * 256], KD, 256, 0, 256)])
                slv = load_slab([(w_ffn_up[:, FFN + jp * 256: FFN + (jp + 1) * 256], KD, 256, 0, 256)])
                for q in range(2):
                    j = jp * 2 + q
                    ut = slot(0, 2 + PT + 2 + ST)
                    pvu, pbu = linear_chunk(slu, KD, 256, q * P, lambda k, a, b_: xT.t[:, k, a:b_], [xT])
                    act(ut[:, 2:2 + HALF], pvu[:, 0, :], AF.Copy, [pbu[0]], [TB_[0]])
                    act(ut[:, 2 + HALF:2 + PT], pvu[:, 1, 0:PT - HALF], AF.Copy, [pbu[1]], [TB_[0]])
                    act(ut[:, 2 + PT + 2:2 + PT + 2 + ST], pvu[:, 1, PT - HALF:HALF], AF.Copy, [pbu[1]], [TB_[0]])
                    cp("act", ut[:, 0:2], cuf[0].t[:, j, :], [cuf[0]], [TB_[0]])
                    cp("act", ut[:, 2 + PT:2 + PT + 2], cuf[1].t[:, j, :], [cuf[1]], [TB_[0]])
                    pvv_, pbv = linear_chunk(slv, KD, 256, q * P, lambda k, a, b_: xT.t[:, k, a:b_], [xT])
                    uc = slot(1, TB)
                    for si, (c0, ln, _) in enumerate(SEGS):
                        z0 = UOFF[si]
                        ts("dve", uc[:, c0:c0 + ln], ut[:, z0:z0 + ln], fcw.t[:, j, 0:1], fcb.t[:, j:j + 1], ALU.mult, ALU.add, [TB_[0], CST], [TB_[1]])
                        for jj in range(1, 3):
                            stt("dve", uc[:, c0:c0 + ln], ut[:, z0 + jj:z0 + jj + ln], fcw.t[:, j, jj:jj + 1], uc[:, c0:c0 + ln], ALU.mult, ALU.add, [TB_[0], TB_[1], CST], [TB_[1]])
                    cp("act", cuf[0].t[:, j, :], ut[:, PT:PT + 2], [TB_[0]], [cuf[0]])
                    cp("act", cuf[1].t[:, j, :], ut[:, 2 + PT + ST:2 + PT + ST + 2], [TB_[0]], [cuf[1]])
                    act(uc[:, 0:TB], uc[:, 0:TB], AF.Gelu_apprx_tanh, [TB_[1]], [TB_[1]])
                    tt("dve", fm2(actT[:, j, :]), pvv_, fm2(uc[:, 0:TB]), ALU.mult, pbv + [TB_[1]], [ACTB[j]])
            S.barrier()
            if b % 2 == 1:
                st_fm3(fconv_s_o[seq], cuf[1].t, [cuf[1]])
            project_tokmajor(w_ffn_down, NF, actT, ACTB)

            def out_dst(t5, b=b):
                if t5 < 4:
                    return y_p[b * PT + t5 * P: b * PT + (t5 + 1) * P, :]
                r0 = (b // 2) * 32 + (b % 2) * ST
                return y_s[r0:r0 + ST, :]
            post_norm_residual(b, gpf, x1_src(b), [X1B[b]], out_dst, [DR])
            S.barrier()
        st_fm3(fconv_p_o, cuf[0].t, [cuf[0]])

        S.lower(nc)
    return nc


_CACHE = {}


def _get_program():
    if "nc" not in _CACHE:
        _CACHE["nc"] = build_program()
    return _CACHE["nc"]


def kernel(x_prompt, x_sample, cache_mem_k, cache_mem_v, state_rnn_h, state_rnn_conv, state_hg,
           state_ffn_conv, mem_prompt, pre_mix_norm, w_in, rnn_conv_w, rnn_conv_b, lru_wa, lru_ba,
           lru_wx, lru_bx, lru_lambda, hg_lb, hg_norm, mem_norm, w_mem_kv, w_branch, b_gate, w_out,
           post_mix_norm, pre_ffn_norm, w_ffn_up, ffn_conv_w, ffn_conv_b, w_ffn_down, post_ffn_norm):
    nc = _get_program()
    in_maps = make_in_maps(x_prompt, x_sample, cache_mem_k, cache_mem_v, state_rnn_h, state_rnn_conv, state_hg,
                           state_ffn_conv, mem_prompt, pre_mix_norm, w_in, rnn_conv_w, rnn_conv_b, lru_wa, lru_ba,
                           lru_wx, lru_bx, lru_lambda, hg_lb, hg_norm, mem_norm, w_mem_kv, w_branch, b_gate, w_out,
                           post_mix_norm, pre_ffn_norm, w_ffn_up, ffn_conv_w, ffn_conv_b, w_ffn_down, post_ffn_norm)
    res = run_bass_kernel_spmd(nc, in_maps, core_ids=list(range(NCORES)))
    return assemble(res.results)


def make_in_maps(x_prompt, x_sample, cache_mem_k, cache_mem_v, state_rnn_h, state_rnn_conv, state_hg,
                 state_ffn_conv, mem_prompt, pre_mix_norm, w_in, rnn_conv_w, rnn_conv_b, lru_wa, lru_ba,
                 lru_wx, lru_bx, lru_lambda, hg_lb, hg_norm, mem_norm, w_mem_kv, w_branch, b_gate, w_out,
                 post_mix_norm, pre_ffn_norm, w_ffn_up, ffn_conv_w, ffn_conv_b, w_ffn_down, post_ffn_norm):
    f = lambda a: np.ascontiguousarray(np.asarray(a, dtype=np.float32))
    x_prompt = f(x_prompt); x_sample = f(x_sample)
    shared = {
        "pre_mix_norm": f(pre_mix_norm).reshape(1, D), "w_in": f(w_in)[0], "rnn_conv_w": f(rnn_conv_w)[0],
        "rnn_conv_b": f(rnn_conv_b).reshape(1, 1024), "lru_wa": f(lru_wa)[0], "lru_ba": f(lru_ba).reshape(1, 1024),
        "lru_wx": f(lru_wx)[0], "lru_bx": f(lru_bx).reshape(1, 1024), "lru_lambda": f(lru_lambda).reshape(1, 1024),
        "hg_lb": f(hg_lb), "hg_norm": f(hg_norm).reshape(1, 128), "mem_norm": f(mem_norm).reshape(1, D),
        "w_mem_kv": f(w_mem_kv)[0], "w_branch": f(w_branch)[0], "b_gate": f(b_gate)[0], "w_out": f(w_out)[0],
        "post_mix_norm": f(post_mix_norm).reshape(1, D), "pre_ffn_norm": f(pre_ffn_norm).reshape(1, D),
        "w_ffn_up": f(w_ffn_up)[0], "ffn_conv_w": f(ffn_conv_w)[0], "ffn_conv_b": f(ffn_conv_b).reshape(1, FFN),
        "w_ffn_down": f(w_ffn_down)[0], "post_ffn_norm": f(post_ffn_norm).reshape(1, D),
    }
    ckk = f(cache_mem_k)[0].reshape(16, 256, 1024)
    cvv = f(cache_mem_v)[0].reshape(16, 256, 1024)
    in_maps = []
    for c in range(NCORES):
        bi, qi = c // 4, c % 4
        t0 = qi * 2048
        m = dict(shared)
        m["xp"] = x_prompt[bi, t0:t0 + 2048]
        m["xs"] = x_sample[2 * c:2 * c + 2].reshape(64, D)
        xh = np.zeros((3, D), np.float32)
        if qi > 0:
            xh[:] = x_prompt[bi, t0 - 3:t0]
        m["xhalo"] = xh
        m["mem"] = f(mem_prompt)[bi]
        m["ck"] = ckk[2 * c:2 * c + 2]
        m["cv"] = cvv[2 * c:2 * c + 2]
        m["st_h"] = f(state_rnn_h)[0, 2 * c:2 * c + 2]
        m["st_conv"] = f(state_rnn_conv)[0, 2 * c:2 * c + 2]
        m["st_hg"] = f(state_hg)[0, 2 * c:2 * c + 2]
        m["st_fconv"] = f(state_ffn_conv)[0, 2 * c:2 * c + 2]
        oh = np.zeros((P, NCORES), np.float32); oh[:, c] = 1.0
        pm = np.zeros((P, NCORES), np.float32); pm[:, bi * 4:c] = 1.0
        p1 = np.zeros((P, NCORES), np.float32)
        if qi > 0:
            p1[:, c - 1] = 1.0
        m["onehot"] = oh; m["predm"] = pm; m["prev1"] = p1
        in_maps.append({k: np.ascontiguousarray(v) for k, v in m.items()})
    return in_maps


def assemble(R):
    y_prompt = np.stack([np.concatenate([R[b * 4 + q]["y_p"] for q in range(4)], 0) for b in range(2)], 0)
    y_sample = np.concatenate([R[c]["y_s"].reshape(2, 32, D) for c in range(NCORES)], 0)
    mem_k = np.stack([R[b * 4]["mk_o"].reshape(256, 4, 256) for b in range(2)], 0)[None]
    mem_v = np.stack([R[b * 4]["mv_o"].reshape(256, 4, 256) for b in range(2)], 0)[None]
    rnn_h_p = np.stack([R[b * 4 + 3]["h_p_o"].reshape(1024) for b in range(2)], 0)[None]
    rnn_conv_p = np.stack([R[b * 4 + 3]["conv_p_o"] for b in range(2)], 0)[None]
    hg_p = np.stack([R[b * 4 + 3]["hg_p_o"] for b in range(2)], 0)[None]
    fconv_p = np.stack([R[b * 4 + 3]["fconv_p_o"] for b in range(2)], 0)[None]
    rnn_h_s = np.concatenate([R[c]["h_s_o"] for c in range(NCORES)], 0)[None]
    rnn_conv_s = np.concatenate([R[c]["conv_s_o"] for c in range(NCORES)], 0)[None]
    hg_s = np.concatenate([R[c]["hg_s_o"] for c in range(NCORES)], 0)[None]
    fconv_s = np.concatenate([R[c]["fconv_s_o"] for c in range(NCORES)], 0)[None]
    outs = (y_prompt, y_sample, mem_k, mem_v, rnn_h_p, rnn_conv_p, hg_p, fconv_p, rnn_h_s, rnn_conv_s, hg_s, fconv_s)
    return tuple(np.ascontiguousarray(o, dtype=np.float32) for o in outs)
```
